# Optimizing a Trainium2 kernel written in Bass

```python
import math
import jax, jax.numpy as jnp
from jax import lax
import numpy as np

D_MODEL = 1024
BATCH = 8
SEQ = 2048
DEPTH = 4

D_FF = 2816
FFN_HALF = 0.5
DN_HEAD_DIM = 128
DN_HEADS = D_MODEL // 128
DN_WIDTH = DN_HEADS * DN_HEAD_DIM
DN_CONV = 4
DN_CHUNK = 64
SB_HEAD_DIM = 128
SB_HEADS = D_MODEL // 128
SB_WIDTH = SB_HEADS * SB_HEAD_DIM
SB_BLOCK = 128
IN_SIZES = (3 * DN_WIDTH, DN_WIDTH, DN_HEADS, DN_HEADS, SB_WIDTH, SB_WIDTH, SB_WIDTH, D_MODEL, D_MODEL)
N_IN = 4 * DN_WIDTH + 2 * DN_HEADS + 3 * SB_WIDTH + 2 * D_MODEL
RMS_EPS = 1e-6
L2_EPS = 1e-6

kernel_name = "hybrid_deltanet_stickbreaking_macaron"


def rmsnorm(x, gain):
    xf = x.astype(jnp.float32)
    y = xf * lax.rsqrt(jnp.mean(xf * xf, axis=-1, keepdims=True) + RMS_EPS)
    return (y * gain.astype(jnp.float32)).astype(x.dtype)


def l2norm(x):
    xf = x.astype(jnp.float32)
    return xf * lax.rsqrt(jnp.sum(xf * xf, axis=-1, keepdims=True) + L2_EPS)


def swiglu(h, w_in, w_out):
    gate, up = jnp.split(h @ w_in, 2, axis=-1)
    return (jax.nn.silu(gate) * up) @ w_out


def causal_depthwise_conv(x, w):
    width, ch = w.shape
    return lax.conv_general_dilated(
        x, w[:, None, :], window_strides=(1,), padding=((width - 1, 0),),
        dimension_numbers=("NWC", "WIO", "NWC"), feature_group_count=ch)


def split_cols(proj):
    parts, start = [], 0
    for size in IN_SIZES:
        parts.append(proj[..., start:start + size])
        start += size
    return parts


def to_heads(t, n_heads, head_dim):
    b, s, _ = t.shape
    return t.reshape(b, s, n_heads, head_dim).transpose(0, 2, 1, 3)


def gated_delta_rule_chunked(q, k, v, g, beta):
    b, h, t, dk = q.shape
    dv = v.shape[-1]
    c = DN_CHUNK
    n = t // c
    q = (q * (dk ** -0.5)).reshape(b, h, n, c, dk)
    k = k.reshape(b, h, n, c, dk)
    v = v.astype(jnp.float32).reshape(b, h, n, c, dv)
    beta = beta.reshape(b, h, n, c)
    g = lax.cumsum(g.reshape(b, h, n, c), axis=3)
    idx = jnp.arange(c)
    lower_incl = idx[:, None] >= idx[None, :]
    strict = idx[:, None] > idx[None, :]
    decay = jnp.exp(jnp.where(lower_incl, g[..., :, None] - g[..., None, :], -jnp.inf))
    k_beta = k * beta[..., None]
    lmat = jnp.where(strict, jnp.einsum("bhnid,bhnjd->bhnij", k_beta, k) * decay, 0.0)
    amat = lmat + jnp.eye(c, dtype=jnp.float32)
    rhs = jnp.concatenate([v * beta[..., None], k_beta * jnp.exp(g)[..., None]], axis=-1)
    sol = lax.linalg.triangular_solve(amat, rhs, left_side=True, lower=True, unit_diagonal=True)
    u, w = sol[..., :dv], sol[..., dv:]
    attn_intra = jnp.einsum("bhnid,bhnjd->bhnij", q, k) * decay
    q_dec = q * jnp.exp(g)[..., None]
    g_last = g[..., -1]
    k_dec = k * jnp.exp(g_last[..., None] - g)[..., None]

    def chunk_step(state, inp):
        q_i, k_i, u_i, w_i, a_i, gl_i = inp
        v_new = u_i - jnp.einsum("bhcd,bhdv->bhcv", w_i, state)
        o_i = jnp.einsum("bhcd,bhdv->bhcv", q_i, state) + jnp.einsum("bhij,bhjv->bhiv", a_i, v_new)
        state = state * jnp.exp(gl_i)[..., None, None] + jnp.einsum("bhcd,bhcv->bhdv", k_i, v_new)
        return state, o_i

    chunk_first = lambda arr: jnp.moveaxis(arr, 2, 0)
    s0 = jnp.zeros((b, h, dk, dv), jnp.float32)
    _, o = lax.scan(chunk_step, s0, (chunk_first(q_dec), chunk_first(k_dec), chunk_first(u),
                                     chunk_first(w), chunk_first(attn_intra), chunk_first(g_last)))
    return jnp.moveaxis(o, 0, 2).reshape(b, h, t, dv)


def stick_breaking_attention(q, k, v):
    t = q.shape[2]
    scale = q.shape[-1] ** -0.5
    outs = []
    for blk in range(t // SB_BLOCK):
        t0, t1 = blk * SB_BLOCK, (blk + 1) * SB_BLOCK
        z = jnp.einsum("bhqd,bhkd->bhqk", q[:, :, t0:t1], k[:, :, :t1]).astype(jnp.float32) * scale
        causal = jnp.arange(t1)[None, :] < (t0 + jnp.arange(SB_BLOCK))[:, None]
        log_1mb = jnp.where(causal, -jax.nn.softplus(z), 0.0)
        survive = lax.cumsum(log_1mb, axis=3, reverse=True) - log_1mb
        weights = jnp.where(causal, jnp.exp(jax.nn.log_sigmoid(z) + survive), 0.0)
        outs.append(jnp.einsum("bhqk,bhkd->bhqd", weights.astype(v.dtype), v[:, :, :t1]))
    return jnp.concatenate(outs, axis=2)


def hybrid_mixer(h, w_in, conv_w, a_log, dt_bias, dn_out_norm, sb_q_norm, sb_k_norm,
                 w_branch_a, w_branch_b, w_out):
    b, t, _ = h.shape
    dn_qkv, dn_z, dn_b, dn_a, sb_q, sb_k, sb_v, gate_a, gate_b = split_cols(h @ w_in)
    qkv = jax.nn.silu(causal_depthwise_conv(dn_qkv, conv_w))
    q, k, v = jnp.split(qkv, 3, axis=-1)
    q = l2norm(to_heads(q, DN_HEADS, DN_HEAD_DIM))
    k = l2norm(to_heads(k, DN_HEADS, DN_HEAD_DIM))
    v = to_heads(v, DN_HEADS, DN_HEAD_DIM)
    beta = jax.nn.sigmoid(dn_b.astype(jnp.float32)).transpose(0, 2, 1)
    g = (-jnp.exp(a_log.astype(jnp.float32))
         * jax.nn.softplus(dn_a.astype(jnp.float32) + dt_bias.astype(jnp.float32))).transpose(0, 2, 1)
    o_a = gated_delta_rule_chunked(q, k, v, g, beta).transpose(0, 2, 1, 3)
    o_a = rmsnorm(o_a, dn_out_norm) * jax.nn.silu(dn_z.reshape(b, t, DN_HEADS, DN_HEAD_DIM).astype(jnp.float32))
    y_a = o_a.reshape(b, t, DN_WIDTH).astype(h.dtype) @ w_branch_a
    qb = rmsnorm(to_heads(sb_q, SB_HEADS, SB_HEAD_DIM), sb_q_norm)
    kb = rmsnorm(to_heads(sb_k, SB_HEADS, SB_HEAD_DIM), sb_k_norm)
    vb = to_heads(sb_v, SB_HEADS, SB_HEAD_DIM)
    o_b = stick_breaking_attention(qb, kb, vb).transpose(0, 2, 1, 3).reshape(b, t, SB_WIDTH)
    y_b = o_b @ w_branch_b
    merged = jax.nn.sigmoid(gate_a) * y_a + jax.nn.sigmoid(gate_b) * y_b
    return merged @ w_out


def setup_inputs(seed: int = 0) -> dict:
    key = jax.random.key(seed)
    ks = jax.random.split(key, 20)
    L = DEPTH

    def dense(k, shape, fan_in):
        return jax.random.normal(k, shape, jnp.float32) * (fan_in ** -0.5)

    def gain(k, shape):
        return 1.0 + 0.1 * jax.random.normal(k, shape, jnp.float32)

    dt = jnp.exp(jax.random.uniform(ks[8], (L, DN_HEADS), jnp.float32, math.log(1e-3), math.log(1e-1)))
    return {
        "x": jax.random.normal(ks[0], (BATCH, SEQ, D_MODEL), jnp.float32),
        "ffn1_norm": gain(ks[1], (L, D_MODEL)),
        "ffn1_w_in": dense(ks[2], (L, D_MODEL, 2 * D_FF), D_MODEL),
        "ffn1_w_out": dense(ks[3], (L, D_FF, D_MODEL), D_FF),
        "mix_norm": gain(ks[4], (L, D_MODEL)),
        "w_in": dense(ks[5], (L, D_MODEL, N_IN), D_MODEL),
        "dn_conv_w": dense(ks[6], (L, DN_CONV, 3 * DN_WIDTH), DN_CONV),
        "dn_a_log": jnp.log(jax.random.uniform(ks[7], (L, DN_HEADS), jnp.float32, 1.0, 16.0)),
        "dn_dt_bias": dt + jnp.log(-jnp.expm1(-dt)),
        "dn_out_norm": gain(ks[9], (L, DN_HEAD_DIM)),
        "sb_q_norm": gain(ks[10], (L, SB_HEAD_DIM)),
        "sb_k_norm": gain(ks[11], (L, SB_HEAD_DIM)),
        "w_branch_a": dense(ks[12], (L, DN_WIDTH, D_MODEL), DN_WIDTH),
        "w_branch_b": dense(ks[13], (L, SB_WIDTH, D_MODEL), SB_WIDTH),
        "w_out": dense(ks[14], (L, D_MODEL, D_MODEL), D_MODEL),
        "ffn2_norm": gain(ks[15], (L, D_MODEL)),
        "ffn2_w_in": dense(ks[16], (L, D_MODEL, 2 * D_FF), D_MODEL),
        "ffn2_w_out": dense(ks[17], (L, D_FF, D_MODEL), D_FF),
    }


def reference(x, ffn1_norm, ffn1_w_in, ffn1_w_out, mix_norm, w_in, dn_conv_w, dn_a_log, dn_dt_bias,
              dn_out_norm, sb_q_norm, sb_k_norm, w_branch_a, w_branch_b, w_out,
              ffn2_norm, ffn2_w_in, ffn2_w_out):
    for l in range(DEPTH):
        x = x + FFN_HALF * swiglu(rmsnorm(x, ffn1_norm[l]), ffn1_w_in[l], ffn1_w_out[l])
        x = x + hybrid_mixer(rmsnorm(x, mix_norm[l]), w_in[l], dn_conv_w[l], dn_a_log[l], dn_dt_bias[l],
                             dn_out_norm[l], sb_q_norm[l], sb_k_norm[l],
                             w_branch_a[l], w_branch_b[l], w_out[l])
        x = x + FFN_HALF * swiglu(rmsnorm(x, ffn2_norm[l]), ffn2_w_in[l], ffn2_w_out[l])
    return x
```

```python
import numpy as np
from contextlib import ExitStack
import concourse.bass as bass
import concourse.mybir as mybir
from concourse.bass_utils import run_bass_kernel_spmd

F32 = mybir.dt.float32
BF16 = mybir.dt.bfloat16
ALU = mybir.AluOpType
AF = mybir.ActivationFunctionType

D = 1024
T = 2048
DEPTH = 4
DFF = 2816
NFF = DFF // 128
NIN = 9232
EPS = 1e-6


class _Node:
    __slots__ = ("w", "r", "ch")

    def __init__(self):
        self.w = None
        self.r = {}
        self.ch = {}


class Op:
    __slots__ = ("id", "eng", "fn", "deps", "dma", "lane", "sig", "val", "stream")


class Prog:
    COMPUTE = ("pe", "dve", "act", "pool")
    ENGS = ("pe", "dve", "act", "pool", "sp")

    def __init__(self, nc, es, n_lanes=24):
        self.nc = nc
        self.ops = []
        self.root = _Node()
        self.n_lanes = n_lanes
        self.lane_last = [None] * n_lanes
        self.lane_cnt = [0] * n_lanes
        self.n_dma = 0
        self.n_dma_sw = 0
        self.flushed = 0
        self.sem = {e: es.enter_context(nc.semaphore("s_" + e)) for e in self.COMPUTE}
        self.lsem = [es.enter_context(nc.semaphore("l_%d" % i)) for i in range(n_lanes)]
        self.cnt = {e: 0 for e in self.COMPUTE}
        self.seen = {e: {} for e in self.ENGS}
        self.last_stream = {}
        self.nblk = 0

    def _walk(self, key):
        node = self.root
        anc = []
        for k in key:
            nxt = node.ch.get(k)
            if nxt is None:
                nxt = _Node()
                node.ch[k] = nxt
            node = nxt
            anc.append(node)
        me = anc[-1]
        desc = []
        stack = list(me.ch.values())
        while stack:
            n = stack.pop()
            desc.append(n)
            stack.extend(n.ch.values())
        return anc[:-1], me, desc

    def add(self, eng, fn, reads=(), writes=(), dma=False):
        reads = [k[:2] if k[0] == "ps" else k for k in reads]
        writes = [k[:2] if k[0] == "ps" else k for k in writes]
        op = Op()
        op.id = len(self.ops)
        op.eng = eng
        op.fn = fn
        op.dma = dma
        op.sig = dma
        op.val = None
        op.lane = None
        deps = {}

        def dep(d, kind):
            if d is None:
                return
            deps.setdefault(d, set()).add(kind)

        if dma:
            half = self.n_lanes // 2
            if eng == "pool":
                op.lane = half + self.n_dma_sw % half
                self.n_dma_sw += 1
            else:
                op.lane = self.n_dma % half
                self.n_dma += 1
            dep(self.lane_last[op.lane], "LANE")
            self.lane_last[op.lane] = op.id
            self.lane_cnt[op.lane] += 1
            op.val = 16 * self.lane_cnt[op.lane]
            op.stream = ("lane", op.lane)
        else:
            op.stream = eng
        for key in reads:
            anc, me, desc = self._walk(key)
            for n in anc + [me] + desc:
                dep(n.w, "RAW")
                if key[0] == "ps":
                    for st, d in n.r.items():
                        if st != op.stream:
                            dep(d, "RAR")
        for key in writes:
            anc, me, desc = self._walk(key)
            for n in anc + [me] + desc:
                dep(n.w, "WAW")
                for d in n.r.values():
                    dep(d, "WAR")
        for key in reads:
            anc, me, desc = self._walk(key)
            me.r[op.stream] = op.id
        for key in writes:
            anc, me, desc = self._walk(key)
            me.w = op.id
            me.r = {}
            for n in desc:
                n.w = None
                n.r = {}
        deps.pop(op.id, None)
        op.deps = deps
        self.ops.append(op)
        if fn is not None:
            self.last_stream[op.stream] = op.id
        return op

    def barrier(self):
        last = dict(self.last_stream)
        for e in self.ENGS:
            op = self.add(e, None)
            for s, d in last.items():
                if d != op.id:
                    op.deps.setdefault(d, set()).add("BAR")

    def _needs_wait(self, op, d, kinds):
        if d < self.flushed:
            return False
        dop = self.ops[d]
        if dop.dma or op.dma:
            return True
        if dop.eng != op.eng:
            return True
        if op.eng == "pe":
            return False
        if op.eng == "pool":
            return True
        return ("RAW" in kinds) or ("BAR" in kinds) or ("WAW" in kinds)

    def flush(self):
        self.barrier()
        ops = self.ops[self.flushed:]
        if not ops:
            return
        for op in ops:
            for d, kinds in op.deps.items():
                if self._needs_wait(op, d, kinds):
                    dop = self.ops[d]
                    if not dop.dma and dop.val is None and dop.id >= self.flushed:
                        dop.sig = True
                    elif not dop.dma and dop.val is None:
                        raise RuntimeError("dep on unsignalled flushed op")
        for op in ops:
            if not op.dma and op.sig and op.fn is not None:
                self.cnt[op.eng] += 1
                op.val = self.cnt[op.eng]
        nc = self.nc
        per = {e: [o for o in ops if o.eng == e] for e in self.ENGS}
        self.nblk += 1
        with nc.Block() as block:
            def emit(engname, eng):
                seen = self.seen[engname]
                for op in per[engname]:
                    for d, kinds in op.deps.items():
                        if not self._needs_wait(op, d, kinds):
                            continue
                        dop = self.ops[d]
                        if dop.dma:
                            sem, key = self.lsem[dop.lane], ("l", dop.lane)
                        else:
                            sem, key = self.sem[dop.eng], dop.eng
                        if dop.val is None:
                            raise RuntimeError("dep without value: op %d -> %d" % (op.id, d))
                        if seen.get(key, 0) >= dop.val:
                            continue
                        seen[key] = dop.val
                        eng.wait_ge(sem, dop.val)
                    if op.fn is None:
                        continue
                    ins = op.fn(eng)
                    if op.dma:
                        ins.then_inc(self.lsem[op.lane], 16)
                    elif op.sig:
                        ins.then_inc(self.sem[op.eng], 1)

            if per["pe"]:
                @block.tensor
                def _(e):
                    emit("pe", e)
            if per["dve"]:
                @block.vector
                def _(e):
                    emit("dve", e)
            if per["act"]:
                @block.scalar
                def _(e):
                    emit("act", e)
            if per["pool"]:
                @block.gpsimd
                def _(e):
                    emit("pool", e)
            if per["sp"]:
                @block.sync
                def _(e):
                    emit("sp", e)
        self.flushed = len(self.ops)

    def finish(self):
        self.flush()


def merge_ops(a, b):
    na, nb = len(a), len(b)
    out = []
    i = j = 0
    while i < na or j < nb:
        if j >= nb or (i < na and i * max(nb, 1) <= j * max(na, 1)):
            out.append(a[i])
            i += 1
        else:
            out.append(b[j])
            j += 1
    return out


def merge_n(lists):
    pos = [0] * len(lists)
    out = []
    total = sum(len(x) for x in lists)
    while len(out) < total:
        best, bf = None, None
        for k, lst in enumerate(lists):
            if pos[k] < len(lst):
                f = pos[k] / float(len(lst))
                if bf is None or f < bf:
                    best, bf = k, f
        out.append(lists[best][pos[best]])
        pos[best] += 1
    return out


class XAlias:
    def __init__(self, xT):
        self.f = xT
        self.b = xT.bitcast(BF16)
        self.reset()

    def reset(self):
        self.row = 0
        self.off = 0

    def alloc(self, n, dt, row=None, off=None):
        nbytes = n * (4 if dt == F32 else 2)
        if row is None:
            if self.off + nbytes > 8192:
                self.row += 1
                self.off = 0
            assert self.row < 6, "alias rows exhausted"
            row, off = self.row, self.off
            self.off += nbytes
        if dt == F32:
            return self.f[:, row, off // 4:off // 4 + n]
        return self.b[:, row, off // 2:off // 2 + n]


class Builder:
    def __init__(self, layers=DEPTH, stages=("ffn1", "mix", "ffn2"), heads=8, debug=False, cut=""):
        self.cut = cut
        self.dn_steps = 3
        self.layers = layers
        self.stages = stages
        self.heads = heads
        self.debug = debug
        self.dbg_names = []
        self._uid = 0
        self._rec = None
        self.nc = bass.Bass("TRN2", target_bir_lowering=False)

    def padd(self, *a, **kw):
        if self._rec is not None:
            self._rec.append((a, kw))
        else:
            self.P.add(*a, **kw)

    def sb(self, name, shape, dt):
        self._uid += 1
        return self.nc.sbuf_tensor("%s_u%d" % (name, self._uid), shape, dt)

    def dram_in(self, name, shape):
        return self.nc.dram_tensor(name, list(shape), F32, kind="ExternalInput")

    def build(self):
        nc = self.nc
        L = DEPTH
        self.x_in = self.dram_in("x", [T, D])
        self.w = {}
        for name, shape in [
            ("ffn1_norm", [L, D]), ("ffn1_w_in", [L, D, 2 * DFF]), ("ffn1_w_out", [L, DFF, D]),
            ("mix_norm", [L, D]), ("w_in", [L, D, NIN]), ("dn_conv_w", [L, 4, 3 * D]),
            ("dn_a_log", [L, 8]), ("dn_dt_bias", [L, 8]), ("dn_out_norm", [L, 128]),
            ("sb_q_norm", [L, 128]), ("sb_k_norm", [L, 128]),
            ("w_branch_a", [L, D, D]), ("w_branch_b", [L, D, D]), ("w_out", [L, D, D]),
            ("ffn2_norm", [L, D]), ("ffn2_w_in", [L, D, 2 * DFF]), ("ffn2_w_out", [L, DFF, D]),
        ]:
            self.w[name] = self.dram_in(name, shape)
        self.out = nc.dram_tensor("out", [T, D], F32, kind="ExternalOutput")
        skind = "ExternalOutput" if self.debug else "Internal"
        self.oa_scr = nc.dram_tensor("oa_scr", [D, T], BF16, kind=skind)
        self.ob_scr = nc.dram_tensor("ob_scr", [D, T], BF16, kind=skind)
        self.x_scr = nc.dram_tensor("x_scr", [D, T], F32, kind="Internal")

        with ExitStack() as es:
            P = self.P = Prog(nc, es)
            self.es = es
            self.xT = es.enter_context(self.sb("xT", [128, 8, T], F32))
            self.ones = es.enter_context(self.sb("ones", [128, 128], BF16))
            self.ident = es.enter_context(self.sb("ident", [128, 128], F32))
            self.gains = es.enter_context(self.sb("gains", [128, 3, L, 8], F32))
            self.ps = [es.enter_context(nc.psum_tensor("ps%d" % i, [128, 512], F32)) for i in range(8)]
            self.epsb = es.enter_context(self.sb("epsb", [128, 1], F32))
            self.setup_consts()
            if "mix" in self.stages:
                self.setup_mixer_consts(es)
            self.load_x()
            P.flush()
            for l in range(self.layers):
                if "ffn1" in self.stages:
                    self.ffn(l, 0)
                if "mix" in self.stages:
                    self.mixer(l)
                if "ffn2" in self.stages:
                    self.ffn(l, 2)
            self.store_x()
            P.finish()
        return nc

    def setup_consts(self):
        P, nc = self.P, self.nc
        ones, ident, gains = self.ones, self.ident, self.gains
        P.add("pool", lambda e: e.memset(ones[:], 1.0), writes=[("ones",)])
        P.add("pool", lambda e: e.memset(self.epsb[:], EPS), writes=[("epsb",)])
        P.add("pool", lambda e: e.memset(ident[:], 1.0), writes=[("ident",)])
        P.add("pool", lambda e: e.affine_select(
            out=ident[:], in_=ident[:], pattern=[[-1, 128]], compare_op=ALU.is_equal,
            fill=0.0, base=0, channel_multiplier=1), reads=[("ident",)], writes=[("ident",)])
        for i, nm in enumerate(("ffn1_norm", "mix_norm", "ffn2_norm")):
            src = self.w[nm].ap().rearrange("l (c p) -> p l c", p=128)
            with nc.allow_non_contiguous_dma(reason="tiny gain load"):
                pass
            P.add("sp", (lambda e, i=i, src=src: e.dma_start(
                out=gains[:, i, :, :], in_=src, allow_slow_non_contiguous=True)),
                writes=[("gains", i)], dma=True)

    def load_x(self):
        P = self.P
        with ExitStack() as es:
            nc = self.nc
            xin = [es.enter_context(self.sb("xin%d" % i, [128, D], F32)) for i in range(3)]
            for tb in range(T // 128):
                b = xin[tb % 3]
                P.add("sp", (lambda e, b=b, tb=tb: e.dma_start(out=b[:], in_=self.x_in.ap()[tb * 128:(tb + 1) * 128, :])),
                      writes=[("xin", tb % 3)], dma=True)
                for g in range(2):
                    for cc in range(4):
                        c = g * 4 + cc
                        pst = self.ps[(tb * 2 + g) % 8]
                        P.add("pe", (lambda e, pst=pst, b=b, c=c, cc=cc: e.transpose(
                            out=pst[:, cc * 128:(cc + 1) * 128], in_=b[:, c * 128:(c + 1) * 128], identity=self.ident[:])),
                            reads=[("xin", tb % 3), ("ident",)], writes=[("ps", (tb * 2 + g) % 8, cc)])
                    pst = self.ps[(tb * 2 + g) % 8]
                    eng = "dve" if g == 0 else "act"
                    dst = self.xT[:, g * 4:(g + 1) * 4, tb * 128:(tb + 1) * 128]
                    srcv = pst[:].rearrange("p (c t) -> p c t", c=4)
                    if eng == "dve":
                        P.add("dve", (lambda e, dst=dst, srcv=srcv: e.tensor_copy(out=dst, in_=srcv)),
                              reads=[("ps", (tb * 2 + g) % 8)], writes=[("xT", g * 4 + k, tb) for k in range(4)])
                    else:
                        P.add("act", (lambda e, dst=dst, srcv=srcv: e.copy(out=dst, in_=srcv)),
                              reads=[("ps", (tb * 2 + g) % 8)], writes=[("xT", g * 4 + k, tb) for k in range(4)])
            P.flush()

    def store_x(self):
        P = self.P
        with ExitStack() as es:
            nc = self.nc
            xo = [es.enter_context(self.sb("xo%d" % i, [128, D], F32)) for i in range(3)]
            for tb in range(T // 128):
                b = xo[tb % 3]
                for g in range(2):
                    for cc in range(4):
                        c = g * 4 + cc
                        pst = self.ps[(tb * 2 + g) % 8]
                        P.add("pe", (lambda e, pst=pst, c=c, cc=cc, tb=tb: e.transpose(
                            out=pst[:, cc * 128:(cc + 1) * 128], in_=self.xT[:, c, tb * 128:(tb + 1) * 128],
                            identity=self.ident[:])),
                            reads=[("xT",), ("ident",)], writes=[("ps", (tb * 2 + g) % 8, cc)])
                    pst = self.ps[(tb * 2 + g) % 8]
                    dst = b[:, g * 512:(g + 1) * 512]
                    if g == 0:
                        P.add("dve", (lambda e, dst=dst, pst=pst: e.tensor_copy(out=dst, in_=pst[:])),
                              reads=[("ps", (tb * 2 + g) % 8)], writes=[("xo", tb % 3, g)])
                    else:
                        P.add("act", (lambda e, dst=dst, pst=pst: e.copy(out=dst, in_=pst[:])),
                              reads=[("ps", (tb * 2 + g) % 8)], writes=[("xo", tb % 3, g)])
                P.add("sp", (lambda e, b=b, tb=tb: e.dma_start(out=self.out.ap()[tb * 128:(tb + 1) * 128, :], in_=b[:])),
                      reads=[("xo", tb % 3)], dma=True)
            P.flush()

    def rmsnorm_T(self, hT, which, l, t0, nt, es_tmp):
        P, nc = self.P, self.nc
        sq, rstd = es_tmp
        for tg in range(nt // 512):
            a = t0 + tg * 512
            pb = self.ps[(tg % 2)]
            for c in range(8):
                s = sq[c % 2]
                P.add("act", (lambda e, s=s, c=c, a=a: e.activation(
                    out=s[:], in_=self.xT[:, c, a:a + 512], func=AF.Square)),
                    reads=[("xT", c, a // 128 + k) for k in range(4)],
                    writes=[("sq", c % 2)])
                P.add("pe", (lambda e, pb=pb, s=s, c=c: e.matmul(pb[:], lhsT=self.ones[:], rhs=s[:], start=(c == 0), stop=(c == 7))),
                      reads=[("sq", c % 2), ("ones",)], writes=[("ps", tg % 2)])
            r = rstd[tg % 2]
            P.add("act", (lambda e, r=r, pb=pb: e.activation(
                out=r[:], in_=pb[:], func=AF.Ln, bias=self.epsb[:], scale=1.0 / D)),
                reads=[("ps", tg % 2), ("epsb",)], writes=[("rstd", tg % 2)])
            P.add("act", (lambda e, r=r: e.activation(out=r[:], in_=r[:], func=AF.Exp, scale=-0.5)),
                reads=[("rstd", tg % 2)], writes=[("rstd", tg % 2)])
            for c in range(8):
                P.add("dve", (lambda e, r=r, c=c, a=a, tg=tg: e.scalar_tensor_tensor(
                    out=hT[:, c, tg * 512:(tg + 1) * 512], in0=self.xT[:, c, a:a + 512],
                    scalar=self.gains[:, which, l, c:c + 1], in1=r[:], op0=ALU.mult, op1=ALU.mult)),
                    reads=[("xT", c, a // 128 + k) for k in range(4)] + [("rstd", tg % 2), ("gains", which)],
                    writes=[("hT", c, tg)])

    def ffn(self, l, which):
        P, nc = self.P, self.nc
        pre = "ffn1" if which == 0 else "ffn2"
        w_in = self.w[pre + "_w_in"].ap()[l]
        w_out = self.w[pre + "_w_out"].ap()[l]
        HT = 1024
        G = 4
        groups = [list(range(s, min(s + G, NFF))) for s in range(0, NFF, G)]
        with ExitStack() as es:
            hT = es.enter_context(self.sb("hT", [128, 8, HT], BF16))
            sq = [es.enter_context(self.sb("sq%d" % i, [128, 512], BF16)) for i in range(2)]
            rstd = [es.enter_context(self.sb("rstd%d" % i, [128, 512], F32)) for i in range(2)]
            NW = 3
            wi = [es.enter_context(self.sb("wi%d" % i, [128, 2, 8, 128], BF16)) for i in range(NW)]
            wo = [es.enter_context(self.sb("wo%d" % i, [128, D], BF16)) for i in range(2 * G)]
            actT = [es.enter_context(self.sb("actT%d" % i, [128, HT], BF16)) for i in range(2 * G)]
            sg = [es.enter_context(self.sb("sg%d" % i, [128, 512], F32)) for i in range(2)]
            w_in_v = w_in.rearrange("(c p) (u f) -> p u c f", p=128, u=2)
            w_out_v = w_out.rearrange("(j p) d -> p j d", p=128)

            def load_wi(j, slot):
                for u in range(2):
                    P.add("pool", (lambda e, u=u: e.dma_start(out=wi[slot][:, u, :, :], in_=w_in_v[:, u, :, j * 128:(j + 1) * 128])),
                          writes=[("wi", slot, u)], dma=True)

            def load_wo(j, slot):
                P.add("pool", (lambda e: e.dma_start(out=wo[slot][:], in_=w_out_v[:, j, :])),
                      writes=[("wo", slot)], dma=True)

            for half in range(T // HT):
                t0 = half * HT
                self.rmsnorm_T(hT, which, l, t0, HT, (sq, rstd))
                seq = [(half, j) for j in range(NFF)]
                load_wi(0, 0)
                load_wi(1, 1)
                nsg = 0
                for gi, grp in enumerate(groups):
                    gb = (gi % 2) * G
                    for jj, j in enumerate(grp):
                        slot = j % NW
                        if j + 2 < NFF:
                            load_wi(j + 2, (j + 2) % NW)
                        load_wo(j, gb + jj)
                        for u in range(2):
                            for tg in range(2):
                                bank = 2 + u * 2 + tg
                                for c in range(8):
                                    P.add("pe", (lambda e, bank=bank, slot=slot, c=c, u=u, tg=tg: e.matmul(
                                        self.ps[bank][:], lhsT=wi[slot][:, u, c, :], rhs=hT[:, c, tg * 512:(tg + 1) * 512],
                                        start=(c == 0), stop=(c == 7))),
                                        reads=[("wi", slot, u), ("hT", c, tg)], writes=[("ps", bank)])
                        for tg in range(2):
                            s = sg[nsg % 2]
                            P.add("act", (lambda e, s=s, tg=tg: e.activation(out=s[:], in_=self.ps[2 + tg][:], func=AF.Silu)),
                                  reads=[("ps", 2 + tg)], writes=[("sg", nsg % 2)])
                            P.add("dve", (lambda e, s=s, tg=tg, gb=gb, jj=jj: e.tensor_tensor(
                                out=actT[gb + jj][:, tg * 512:(tg + 1) * 512], in0=s[:], in1=self.ps[4 + tg][:], op=ALU.mult)),
                                reads=[("sg", nsg % 2), ("ps", 4 + tg)], writes=[("actT", gb + jj, tg)])
                            nsg += 1
                    for c in range(8):
                        for tg in range(2):
                            bank = 6 + (c * 2 + tg) % 2
                            for jj, j in enumerate(grp):
                                P.add("pe", (lambda e, bank=bank, gb=gb, jj=jj, c=c, tg=tg, n=len(grp): e.matmul(
                                    self.ps[bank][:], lhsT=wo[gb + jj][:, c * 128:(c + 1) * 128],
                                    rhs=actT[gb + jj][:, tg * 512:(tg + 1) * 512], start=(jj == 0), stop=(jj == n - 1))),
                                    reads=[("wo", gb + jj), ("actT", gb + jj, tg)], writes=[("ps", bank)])
                            a = t0 + tg * 512
                            xs = self.xT[:, c, a:a + 512]
                            P.add("dve", (lambda e, xs=xs, bank=bank: e.scalar_tensor_tensor(
                                out=xs, in0=self.ps[bank][:], scalar=0.5, in1=xs, op0=ALU.mult, op1=ALU.add)),
                                reads=[("ps", bank)] + [("xT", c, a // 128 + k) for k in range(4)],
                                writes=[("xT", c, a // 128 + k) for k in range(4)])
                            if which == 0 and "mix" in self.stages and gi == len(groups) - 1:
                                self.dma("sp", self.x_scr.ap()[c * 128:(c + 1) * 128, a:a + 512], xs,
                                         [("xT", c, a // 128 + k) for k in range(4)], [("x_scr", c, a // 512)])
            P.flush()

    def mm(self, out, lhsT, rhs, start, stop, R, W):
        self.padd("pe", lambda e: e.matmul(out, lhsT=lhsT, rhs=rhs, start=start, stop=stop), reads=R, writes=W)

    def tr(self, out, in_, ident, R, W):
        self.padd("pe", lambda e: e.transpose(out=out, in_=in_, identity=ident), reads=R, writes=W)

    def actf(self, out, in_, func, R, W, bias=None, scale=None):
        kw = {}
        if bias is not None:
            kw["bias"] = bias
        if scale is not None:
            kw["scale"] = scale
        self.padd("act", lambda e: e.activation(out=out, in_=in_, func=func, **kw), reads=R, writes=W)

    def tt(self, eng, out, in0, in1, op, R, W):
        self.padd(eng, lambda e: e.tensor_tensor(out=out, in0=in0, in1=in1, op=op), reads=R, writes=W)

    def ts(self, eng, out, in0, s1, s2, op0, op1, R, W):
        if s2 is None:
            self.padd(eng, lambda e: e.tensor_scalar(out=out, in0=in0, scalar1=s1, scalar2=None, op0=op0), reads=R, writes=W)
        else:
            self.padd(eng, lambda e: e.tensor_scalar(out=out, in0=in0, scalar1=s1, scalar2=s2, op0=op0, op1=op1), reads=R, writes=W)

    def stt(self, eng, out, in0, scalar, in1, op0, op1, R, W):
        self.padd(eng, lambda e: e.scalar_tensor_tensor(out=out, in0=in0, scalar=scalar, in1=in1, op0=op0, op1=op1),
                   reads=R, writes=W)

    def cp(self, eng, out, in_, R, W):
        if eng == "act":
            self.padd("act", lambda e: e.copy(out=out, in_=in_), reads=R, writes=W)
        else:
            self.padd(eng, lambda e: e.tensor_copy(out=out, in_=in_), reads=R, writes=W)

    def dma(self, eng, out, in_, R, W, slow=False):
        if slow:
            self.padd(eng, lambda e: e.dma_start(out=out, in_=in_, allow_slow_non_contiguous=True), reads=R, writes=W, dma=True)
        else:
            self.padd(eng, lambda e: e.dma_start(out=out, in_=in_), reads=R, writes=W, dma=True)

    def sel(self, out, pattern, cmp, fill, base, cm, key):
        self.padd("pool", lambda e: e.affine_select(out=out, in_=out, pattern=pattern, compare_op=cmp, fill=fill,
                                                     base=base, channel_multiplier=cm), reads=[key], writes=[key])

    def memset(self, ap, val, key):
        self.padd("pool", lambda e: e.memset(ap, val), writes=[key])

    def setup_mixer_consts(self, es):
        nc, P = self.nc, self.P
        A = lambda name, shape, dt: es.enter_context(self.sb(name, shape, dt))
        NEG = -30000.0
        self.triU = A("triU", [128, 128], F32)
        self.sel127 = A("sel127", [128, 128], F32)
        self.triI = A("triI", [128, 128], BF16)
        self.SU = A("SU", [128, 128], F32)
        self.NEGlo = A("NEGlo", [128, 128], F32)
        self.NEGui = A("NEGui", [128, 128], F32)
        self.cmask = A("cmask", [128, 512], F32)
        self.mask01 = [A("mask01_%d" % r, [128, 512], BF16) for r in range(4)]
        self.identb = A("identb", [128, 128], BF16)
        self.eps128 = A("eps128", [128, 1], F32)
        self.one1 = A("one1", [128, 1], F32)
        self.memset(self.triU[:], 1.0, ("triU",))
        self.sel(self.triU[:], [[1, 128]], ALU.is_ge, 0.0, 0, -1, ("triU",))
        self.memset(self.sel127[:], 1.0, ("sel127",))
        self.sel(self.sel127[:], [[0, 128]], ALU.is_equal, 0.0, -127, 1, ("sel127",))
        self.memset(self.SU[:], 1.0, ("SU",))
        self.sel(self.SU[:], [[1, 128]], ALU.is_gt, 0.0, 0, -1, ("SU",))
        self.memset(self.NEGlo[:], 0.0, ("NEGlo",))
        self.sel(self.NEGlo[:], [[1, 128]], ALU.is_ge, NEG, 0, -1, ("NEGlo",))
        self.memset(self.NEGui[:], 0.0, ("NEGui",))
        self.sel(self.NEGui[:], [[-1, 128]], ALU.is_gt, NEG, 0, 1, ("NEGui",))
        self.memset(self.cmask[:], 1.0, ("cmask",))
        self.memset(self.cmask[:].rearrange("p (c j) -> p c j", j=128)[:, :, 0:1], 0.0, ("cmask",))
        self.mA = A("mA", [128, 128], BF16)
        self.mB = A("mB", [128, 128], BF16)
        self.mC = A("mC", [128, 128], BF16)
        for (m, bs, key) in ((self.mA, 32, ("mA",)), (self.mC, 64, ("mC",))):
            for b in range(128 // bs):
                cs = m[:, b * bs:(b + 1) * bs]
                self.memset(cs, 1.0, key)
                self.sel(cs, [[0, bs]], ALU.is_ge, 0.0, -bs * b, 1, key)
                self.sel(cs, [[0, bs]], ALU.is_ge, 0.0, bs * b + bs - 1, -1, key)
        self.tt("pool", self.mB[:], self.mC[:], self.mA[:], ALU.subtract, [("mA",), ("mC",)], [("mB",)])
        self.ts("pool", self.mC[:], self.mC[:], -1.0, 1.0, ALU.mult, ALU.add, [("mC",)], [("mC",)])
        self.memset(self.eps128[:], 128.0 * EPS, ("eps128",))
        self.memset(self.one1[:], 1.0, ("one1",))
        self.cp("pool", self.identb[:], self.ident[:], [("ident",)], [("identb",)])
        with ExitStack() as es2:
            tmp = es2.enter_context(self.sb("ctmp", [128, 512], F32))
            self.memset(tmp[:, 0:128], 1.0, ("ctmp",))
            self.sel(tmp[:, 0:128], [[-1, 128]], ALU.is_ge, 0.0, 0, 1, ("ctmp",))
            self.cp("pool", self.triI[:], tmp[:, 0:128], [("ctmp",)], [("triI",)])
            for r in range(4):
                self.memset(tmp[:], 1.0, ("ctmp",))
                self.sel(tmp[:], [[1, 512]], ALU.is_gt, 0.0, -128 * r, -1, ("ctmp",))
                self.cp("pool", self.mask01[r][:], tmp[:], [("ctmp",)], [("mask01", r)])
            P.flush()

    def mixer(self, l):
        P, nc = self.P, self.nc
        w_in_l = self.w["w_in"].ap()[l]
        w_in_v = w_in_l.rearrange("(c p) n -> p c n", p=128)
        with ExitStack() as es:
            A = lambda name, shape, dt: es.enter_context(self.sb(name, shape, dt))
            hT = A("mhT", [128, 8, T], BF16)
            with ExitStack() as es2:
                sq = [es2.enter_context(self.sb("msq%d" % i, [128, 512], BF16)) for i in range(2)]
                rstd = [es2.enter_context(self.sb("mrs%d" % i, [128, 512], F32)) for i in range(2)]
                self.rmsnorm_T(hT, 1, l, 0, T, (sq, rstd))
                P.flush()
            colw = A("colw", [128, 8, 16], F32)
            wba = A("wba", [128, 8, 16], BF16)
            wrep = [A("wrep%d" % i, [128, 8, 128], BF16) for i in range(2)]
            cwraw = A("cwraw", [96, 128], F32)
            cw = A("cw", [128, 96], F32)
            dtb = A("dtb", [128, 8], F32)
            nexpA = A("nexpA", [128, 8], F32)
            gdn = A("gdn", [128, 1], F32)
            gq = A("gq", [128, 1], F32)
            gk = A("gk", [128, 1], F32)
            bcol = A("bcol", [128, 16, 8], F32)
            gcol = A("gcol", [128, 16, 8], F32)
            ngcol = A("ngcol", [128, 16, 8], F32)
            bgc = A("bgc", [128, 16, 8], F32)
            ekd = A("ekd", [128, 16, 8], F32)
            egl = A("egl", [128, 16, 8], F32)
            ctmp = A("lctmp", [128, 16, 8], F32)
            glb = A("glb", [128, 16, 8], F32)
            NWS = 7
            wring = [A("wring%d" % i, [128, 8, 128], BF16) for i in range(NWS)]

            self.dma("sp", colw[:], w_in_v[:, :, 4096:4112], [], [("colw",)])
            self.dma("sp", cwraw[:], self.w["dn_conv_w"].ap()[l].rearrange("i (c p) -> (i c) p", p=128), [], [("cwraw",)])
            self.dma("sp", dtb[:], self.w["dn_dt_bias"].ap()[l].partition_broadcast(128), [], [("dtb",)])
            self.dma("sp", nexpA[:], self.w["dn_a_log"].ap()[l].partition_broadcast(128), [], [("nexpA",)])
            self.dma("sp", gdn[:], self.w["dn_out_norm"].ap()[l].rearrange("(p o) -> p o", o=1), [], [("gdn",)])
            self.dma("sp", gq[:], self.w["sb_q_norm"].ap()[l].rearrange("(p o) -> p o", o=1), [], [("gq",)])
            self.dma("sp", gk[:], self.w["sb_k_norm"].ap()[l].rearrange("(p o) -> p o", o=1), [], [("gk",)])
            self.actf(nexpA[:], nexpA[:], AF.Exp, [("nexpA",)], [("nexpA",)])
            self.ts("dve", nexpA[:], nexpA[:], -1.0, None, ALU.mult, None, [("nexpA",)], [("nexpA",)])
            self.ts("dve", gq[:], gq[:], float(128.0 ** -0.5), None, ALU.mult, None, [("gq",)], [("gq",)])
            self.cp("dve", wba[:], colw[:], [("colw",)], [("wba",)])
            self.tr(self.ps[0][:, 0:96], cwraw[:, :], self.ident[0:96, 0:96], [("cwraw",), ("ident",)], [("ps", 0)])
            self.cp("dve", cw[:], self.ps[0][:, 0:96], [("ps", 0)], [("cw",)])
            for tb in range(16):
                for c in range(8):
                    self.mm(self.ps[1][:, tb * 16:(tb + 1) * 16], hT[:, c, tb * 128:(tb + 1) * 128], wba[:, c, :],
                            c == 0, c == 7, [("hT", c, tb // 4), ("wba",)], [("ps", 1, tb)])
            raw = self.ps[1][:, 0:256].rearrange("p (t q) -> p t q", q=16)
            self.actf(bcol[:], raw[:, :, 0:8], AF.Sigmoid, [("ps", 1)], [("bcol",)])
            self.tt("dve", ctmp[:], raw[:, :, 8:16], dtb[:].unsqueeze(1).to_broadcast([128, 16, 8]), ALU.add,
                    [("ps", 1), ("dtb",)], [("lctmp",)])
            self.actf(ctmp[:], ctmp[:], AF.Exp, [("lctmp",)], [("lctmp",)])
            self.actf(ctmp[:], ctmp[:], AF.Ln, [("lctmp",), ("one1",)], [("lctmp",)], bias=self.one1[:])
            self.tt("dve", ctmp[:], ctmp[:], nexpA[:].unsqueeze(1).to_broadcast([128, 16, 8]), ALU.mult,
                    [("lctmp",), ("nexpA",)], [("lctmp",)])
            c2 = lambda t: t[:].rearrange("p t q -> p (t q)")
            self.mm(self.ps[2][:, 0:128], self.triU[:], c2(ctmp), True, True, [("triU",), ("lctmp",)], [("ps", 2)])
            self.cp("act", c2(gcol), self.ps[2][:, 0:128], [("ps", 2)], [("gcol",)])
            self.padd("act", lambda e: e.mul(out=c2(ngcol), in_=self.ps[2][:, 0:128], mul=-1.0), reads=[("ps", 2)], writes=[("ngcol",)])
            self.mm(self.ps[3][:, 0:128], self.sel127[:], c2(gcol), True, True, [("sel127",), ("gcol",)], [("ps", 3)])
            self.cp("dve", c2(glb), self.ps[3][:, 0:128], [("ps", 3)], [("glb",)])
            self.actf(c2(egl), c2(glb), AF.Exp, [("glb",)], [("egl",)])
            self.tt("dve", c2(ekd), c2(glb), c2(gcol), ALU.subtract, [("glb",), ("gcol",)], [("ekd",)])
            self.actf(c2(ekd), c2(ekd), AF.Exp, [("ekd",)], [("ekd",)])
            self.actf(c2(bgc), c2(gcol), AF.Exp, [("gcol",)], [("bgc",)])
            self.tt("dve", c2(bgc), c2(bgc), c2(bcol), ALU.mult, [("bgc",), ("bcol",)], [("bgc",)])
            if self.debug:
                self.dbg_out("dbg_gcol", c2(gcol), [128, 128], F32, [("gcol",)])
                self.dbg_out("dbg_bcol", c2(bcol), [128, 128], F32, [("bcol",)])
            P.flush()

            def wload(slot, col):
                self.dma("pool", wring[slot][:], w_in_v[:, :, col:col + 128], [], [("wring", slot)])

            DN_COLS = lambda h: [h * 128, 1024 + h * 128, 2048 + h * 128, 3072 + h * 128]
            SB_COLS = lambda h: [4112 + h * 128, 5136 + h * 128, 6160 + h * 128]
            if self.cut == "A":
                return
            if "ffn1" not in self.stages:
                for c in range(8):
                    self.dma("sp", self.x_scr.ap()[c * 128:(c + 1) * 128, :], self.xT[:, c, :], [("xT", c)], [("x_scr", c)])
            P.flush()
            xa = XAlias(self.xT)
            wring2 = [xa.alloc(1024, BF16, row=6 + i // 4, off=(i % 4) * 2048).rearrange("p (k n) -> p k n", n=128) for i in range(NWS)]
            rings = [[w[:] for w in wring], wring2]

            def wload2(ring, slot, col):
                self.dma("pool", rings[ring][slot], w_in_v[:, :, col:col + 128], [], [("wring", ring, slot)])
            for i, col in enumerate(DN_COLS(0) + SB_COLS(0)):
                wload2(0, i, col)
            self._hc = {}
            with ExitStack() as esh:
                def record(gen):
                    rec = []
                    self._rec = rec
                    for _ in gen:
                        pass
                    self._rec = None
                    return rec

                for h in range(self.heads):
                    rg = h % 2
                    xa.reset()
                    rd, rs = [], []
                    if self.cut != "C":
                        rd = record(self.dn_head(esh, xa, l, h, hT, (rg, rings[rg]), colw, wrep, cw, dtb, nexpA, gdn, bcol, gcol, ngcol, bgc, ekd, egl))
                    if self.cut != "B":
                        rs = record(self.sb_head(esh, l, h, hT, (rg, rings[rg]), gq, gk))
                    if h + 1 < self.heads:
                        self._rec = pre = []
                        for i, col in enumerate(DN_COLS(h + 1) + SB_COLS(h + 1)):
                            wload2(1 - rg, i, col)
                        self._rec = None
                        if rd:
                            k = len(rd) // 3
                            rd = rd[:k] + pre + rd[k:]
                        else:
                            k = len(rs) // 3
                            rs = rs[:k] + pre + rs[k:]
                    nd, ns = len(rd), len(rs)
                    i = j = 0
                    while i < nd or j < ns:
                        if j >= ns or (i < nd and i * max(ns, 1) <= j * max(nd, 1)):
                            a, kw = rd[i]
                            i += 1
                        else:
                            a, kw = rs[j]
                            j += 1
                        self.P.add(*a, **kw)
                P.flush()
            xs_v = self.x_scr.ap().rearrange("(c p) t -> p c t", p=128)
            for tg in range(4):
                self.dma("sp", self.xT[:, :, tg * 512:(tg + 1) * 512], xs_v[:, :, tg * 512:(tg + 1) * 512], [("x_scr",)],
                         [("xT", c, tg * 4 + k) for c in range(8) for k in range(4)])
            if self.heads == 8:
                self.mix_out(l, hT)

    def norm_rows(self, src, dst, sq, rtmp, scale, bias_ap, gain_ap, banks, key_src, key_dst, kp=""):
        pb = self.ps[banks]
        self.actf(sq[:], src, AF.Square, [key_src], [(kp + "nsq",)])
        self.mm(pb[:], self.ones[:], sq[:], True, True, [(kp + "nsq",), ("ones",)], [("ps", banks)])
        self.actf(rtmp[:], pb[:], AF.Ln, [("ps", banks)], [(kp + "nrt",)], bias=bias_ap, scale=scale)
        self.actf(rtmp[:], rtmp[:], AF.Exp, [(kp + "nrt",)], [(kp + "nrt",)], scale=-0.5)
        if gain_ap is None:
            self.tt("dve", dst, src, rtmp[:], ALU.mult, [key_src, (kp + "nrt",)], [key_dst])
        else:
            self.stt("dve", dst, src, gain_ap, rtmp[:], ALU.mult, ALU.mult, [key_src, (kp + "nrt",)], [key_dst])

    def dn_head(self, es, xa, l, h, hT, wr, colw, wrep, cw, dtb, nexpA, gdn, bcol, gcol, ngcol, bgc, ekd, egl):
        P, nc = self.P, self.nc
        rg, wring = wr
        if True:
            def A(name, shape, dt):
                k = ("dn", name)
                if k not in self._hc:
                    self._hc[k] = es.enter_context(self.sb(name, shape, dt))
                return self._hc[k]
            B1 = [A("B1_%d" % i, [128, 515], F32) for i in range(3)]
            B2 = [A("B2_%d" % i, [128, 512], F32) for i in range(3)]
            QT = A("QT", [128, 512], BF16)
            KT = A("KT", [128, 512], BF16)
            VT = A("VT", [128, 512], BF16)
            g_row = A("g_row", [128, 512], F32)
            R1 = A("R1", [128, 512], F32)
            R2 = A("R2", [128, 512], F32)
            betaSU = A("betaSU", [128, 512], BF16)
            qdecT_ = [xa.alloc(512, BF16) for _ in range(2)]
            kbg = A("kbg", [128, 512], BF16)
            kdec_ = [xa.alloc(512, BF16) for _ in range(2)]
            bv_ = [xa.alloc(512, BF16) for _ in range(2)]
            Tt_ = [xa.alloc(512, BF16) for _ in range(2)]
            attnT_ = [xa.alloc(512, BF16) for _ in range(2)]
            nwT_ = [xa.alloc(512, BF16) for _ in range(2)]
            zs_ = [xa.alloc(512, BF16) for _ in range(2)]
            oTg = xa.alloc(512, F32)
            R3 = xa.alloc(512, F32)
            R2o = xa.alloc(512, F32)
            sq_o = xa.alloc(512, BF16)
            rtmp_o = xa.alloc(512, F32)
            oab = [A("oab%d" % i, [128, 512], BF16) for i in range(1)]
            sq = A("nsq", [128, 512], BF16)
            rtmp = A("nrt", [128, 512], F32)
            tmp1 = [xa.alloc(128, F32) for _ in range(4)]
            tmp2 = [xa.alloc(128, F32) for _ in range(4)]
            DT = [xa.alloc(128, F32) for _ in range(4)]
            Dm = [xa.alloc(128, F32) for _ in range(4)]
            t1 = [xa.alloc(128, F32) for _ in range(4)]
            Mb = [A("Mb_%d" % i, [128, 512], BF16) for i in range(2)]
            Mtb = [A("Mtb_%d" % i, [128, 512], BF16) for i in range(2)]
            C1 = A("C1", [128, 512], BF16)
            Ct1 = A("Ct1", [128, 512], BF16)
            C2 = A("C2", [128, 512], BF16)
            Ct2 = A("Ct2", [128, 512], BF16)
            X = A("X", [128, 512], BF16)
            Yb = A("Yb", [128, 512], BF16)
            Ytb = A("Ytb", [128, 512], BF16)
            S = A("S", [128, 128], F32)
            Sbf = [A("Sbf_%d" % i, [128, 128], BF16) for i in range(2)]
            vnew = [A("vnew_%d" % i, [128, 128], BF16) for i in range(2)]

            for n in range(3):
                self.memset(B1[n][:, 0:3], 0.0, ("B1", n, "halo"))
            self.memset(S[:], 0.0, ("S",))
            self.memset(Sbf[0][:], 0.0, ("Sbf", 0))
            for qi, colidx in enumerate((h, 8 + h)):
                self.cp("dve", wrep[qi][:], colw[:, :, colidx:colidx + 1].to_broadcast([128, 8, 128]),
                        [("colw",)], [("wrep", qi)])
            nb = [0]

            def bank4():
                nb[0] = (nb[0] + 1) % 2
                return nb[0]

            def record_sub(gen):
                save = self._rec
                self._rec = sub = []
                for _ in gen:
                    pass
                self._rec = save
                return sub

            def emit_ops(ops):
                for a, kw in ops:
                    self.padd(*a, **kw)

            def prep(tg):
                a = tg * 512
                t2 = tg % 2
                qdecT, kdec, bv, Tt, attnT, nwT, zs = qdecT_[t2], kdec_[t2], bv_[t2], Tt_[t2], attnT_[t2], nwT_[t2], zs_[t2]
                def proj(n):
                    b = n
                    for c in range(8):
                        self.mm(self.ps[b][:], wring[n][:, c, :], hT[:, c, a:a + 512], c == 0, c == 7,
                                [("wring", rg, n), ("hT", c, tg)], [("ps", b)])
                    self.cp("act", B1[n][:, 3:515], self.ps[b][:], [("ps", b)], [("B1", n, "main")])
                    acc = B2[n]
                    ch = n * 8 + h
                    self.ts("dve", acc[:], B1[n][:, 3:515], cw[:, 3 * 24 + ch:3 * 24 + ch + 1], None, ALU.mult, None,
                            [("B1", n, "main"), ("cw",)], [("B2", n)])
                    for i in (2, 1, 0):
                        self.stt("dve", acc[:], B1[n][:, i:i + 512], cw[:, i * 24 + ch:i * 24 + ch + 1], acc[:],
                                 ALU.mult, ALU.add, [("B1", n), ("cw",), ("B2", n)], [("B2", n)])
                    if tg < 3:
                        self.cp("pool", B1[n][:, 0:3], B1[n][:, 512:515], [("B1", n, "main")], [("B1", n, "halo")])
                    yield

                emit_ops(merge_n([record_sub(proj(n)) for n in range(3)]))
                bzz = 0
                for c in range(8):
                    self.mm(self.ps[bzz][:], wring[3][:, c, :], hT[:, c, a:a + 512], c == 0, c == 7,
                            [("wring", rg, 3), ("hT", c, tg)], [("ps", bzz)])
                self.actf(B2[0][:], B2[0][:], AF.Silu, [("B2", 0)], [("B2", 0)])
                self.actf(B2[1][:], B2[1][:], AF.Silu, [("B2", 1)], [("B2", 1)])
                self.actf(VT[:], B2[2][:], AF.Silu, [("B2", 2)], [("VT",)])
                self.actf(zs[:], self.ps[bzz][:], AF.Silu, [("ps", bzz)], [("zs", t2)])
                for n in range(2):
                    dst = QT if n == 0 else KT
                    self.norm_rows(B2[n][:], dst[:], sq, rtmp, 128.0 if n == 0 else 1.0,
                                   self.eps128[:] if n == 0 else self.epsb[:], None, 2,
                                   ("B2", n), ("QT",) if n == 0 else ("KT",), kp="dn")
                yield

                def rowb():
                    bb = 0
                    for c in range(8):
                        self.mm(self.ps[bb][:], wrep[0][:, c, :], hT[:, c, a:a + 512], c == 0, c == 7,
                                [("wrep", 0), ("hT", c, tg)], [("ps", bb)])
                    self.actf(R1[:], self.ps[bb][:], AF.Exp, [("ps", bb)], [("R1",)], scale=-1.0)
                    self.actf(R1[:], R1[:], AF.Ln, [("R1",), ("one1",)], [("R1",)], bias=self.one1[:])
                    self.actf(R1[:], R1[:], AF.Exp, [("R1",)], [("R1",)], scale=-1.0)
                    self.tt("dve", betaSU[:].rearrange("p (c j) -> p c j", j=128), R1[:].rearrange("p (c j) -> p c j", j=128),
                            self.SU[:].unsqueeze(1).to_broadcast([128, 4, 128]), ALU.mult, [("R1",), ("SU",)], [("betaSU",)])
                    yield

                def rowg():
                    ba = 1
                    for c in range(8):
                        self.mm(self.ps[ba][:], wrep[1][:, c, :], hT[:, c, a:a + 512], c == 0, c == 7,
                                [("wrep", 1), ("hT", c, tg)], [("ps", ba)])
                    self.actf(R2[:], self.ps[ba][:], AF.Exp, [("ps", ba), ("dtb",)], [("R2",)], bias=dtb[:, h:h + 1])
                    self.actf(R2[:], R2[:], AF.Ln, [("R2",), ("one1",)], [("R2",)], bias=self.one1[:])
                    self.ts("dve", R2[:], R2[:], nexpA[:, h:h + 1], None, ALU.mult, None, [("R2",), ("nexpA",)], [("R2",)])
                    self.padd("dve", lambda e: e.tensor_tensor_scan(out=g_row[:], data0=self.cmask[:], data1=R2[:], initial=0.0,
                                                                    op0=ALU.mult, op1=ALU.add),
                              reads=[("R2",), ("cmask",)], writes=[("g_row",)])
                    self.actf(R3[:], g_row[:], AF.Exp, [("g_row",)], [("R3",)])
                    self.tt("dve", qdecT[:], QT[:], R3[:], ALU.mult, [("QT",), ("R3",)], [("qdecT", t2)])
                    yield

                emit_ops(merge_n([record_sub(rowb()), record_sub(rowg())]))
                psb2 = self.ps[2].bitcast(BF16)

                def chunk(j):
                    c = tg * 4 + j
                    js = slice(j * 128, (j + 1) * 128)
                    self.tr(psb2[:, j * 256:j * 256 + 128], KT[:, js], self.identb[:], [("KT",), ("identb",)], [("ps", 2)])
                    self.tr(psb2[:, j * 256 + 128:j * 256 + 256], VT[:, js], self.identb[:], [("VT",), ("identb",)], [("ps", 2)])
                    self.actf(kbg[:, js], psb2[:, j * 256:j * 256 + 128], AF.Copy, [("ps", 2), ("bgc",)], [("kbg", j)],
                              scale=bgc[:, c, h:h + 1])
                    self.ts("dve", kdec[:, js], psb2[:, j * 256:j * 256 + 128], ekd[:, c, h:h + 1], None, ALU.mult, None,
                            [("ps", 2), ("ekd",)], [("kdec", t2, j)])
                    self.actf(bv[:, js], psb2[:, j * 256 + 128:j * 256 + 256], AF.Copy, [("ps", 2), ("bcol",)], [("bv", t2, j)],
                              scale=bcol[:, c, h:h + 1])
                    pb = j // 2
                    pa = self.ps[pb]
                    c0 = (j % 2) * 256
                    self.mm(pa[:, c0:c0 + 128], KT[:, js], KT[:, js], True, True, [("KT",)], [("ps", pb)])
                    self.mm(pa[:, c0 + 128:c0 + 256], KT[:, js], QT[:, js], True, True, [("KT",), ("QT",)], [("ps", pb)])
                    k2 = j
                    self.tt("pool", tmp1[k2][:], g_row[:, js], self.NEGlo[:], ALU.add, [("g_row",), ("NEGlo",)], [("tmp1", k2)])
                    self.actf(DT[k2][:], tmp1[k2][:], AF.Exp, [("tmp1", k2), ("ngcol",)], [("DT", k2)], bias=ngcol[:, c, h:h + 1])
                    self.tt("pool", tmp2[k2][:], self.NEGui[:], g_row[:, js], ALU.subtract, [("g_row",), ("NEGui",)], [("tmp2", k2)])
                    self.actf(Dm[k2][:], tmp2[k2][:], AF.Exp, [("tmp2", k2), ("gcol",)], [("Dm", k2)], bias=gcol[:, c, h:h + 1])
                    self.tt("dve", t1[k2][:], pa[:, c0:c0 + 128], DT[k2][:], ALU.mult, [("ps", pb), ("DT", k2)], [("t1", k2)])
                    self.tt("pool", Mtb[0][:, js], t1[k2][:], betaSU[:, js], ALU.mult, [("t1", k2), ("betaSU",)], [("Mtb", 0, j)])
                    self.stt("dve", Mb[0][:, js], pa[:, c0:c0 + 128], bcol[:, c, h:h + 1], Dm[k2][:], ALU.mult, ALU.mult,
                             [("ps", pb), ("bcol",), ("Dm", k2)], [("Mb", 0, j)])
                    self.tt("dve", attnT[:, js], pa[:, c0 + 128:c0 + 256], DT[k2][:], ALU.mult, [("ps", pb), ("DT", k2)], [("attnT", t2, j)])
                    yield

                emit_ops(merge_n([record_sub(chunk(j)) for j in range(4)]))
                yield
                v3 = lambda t: t[:].rearrange("p (c j) -> p c j", j=128)
                mb3 = lambda m: m[:].unsqueeze(1).to_broadcast([128, 4, 128])
                Lf, Ltf = Mb[0], Mtb[0]
                self.tt("pool", v3(C1), v3(Lf), mb3(self.mB), ALU.mult, [("Mb", 0), ("mB",)], [("C1",)])
                self.tt("pool", v3(Ct1), v3(Ltf), mb3(self.mB), ALU.mult, [("Mtb", 0), ("mB",)], [("Ct1",)])
                self.tt("pool", v3(C2), v3(Lf), mb3(self.mC), ALU.mult, [("Mb", 0), ("mC",)], [("C2",)])
                self.tt("pool", v3(Ct2), v3(Ltf), mb3(self.mC), ALU.mult, [("Mtb", 0), ("mC",)], [("Ct2",)])
                self.tt("dve", v3(Lf), v3(Lf), mb3(self.mA), ALU.mult, [("Mb", 0), ("mA",)], [("Mb", 0)])
                self.tt("dve", v3(Ltf), v3(Ltf), mb3(self.mA), ALU.mult, [("Mtb", 0), ("mA",)], [("Mtb", 0)])
                self.tt("pool", v3(X), mb3(self.identb), v3(Lf), ALU.subtract, [("Mb", 0), ("identb",)], [("X",)])
                self.tt("pool", v3(Tt), mb3(self.identb), v3(Ltf), ALU.subtract, [("Mtb", 0), ("identb",)], [("Tt", t2)])
                J4 = [slice(j * 128, (j + 1) * 128) for j in range(4)]
                for k in range(1, 5):
                    yield
                    pv_, cu = (k - 1) % 2, k % 2
                    for js in J4:
                        self.mm(self.ps[0][:, js], Mtb[pv_][:, js], Mb[pv_][:, js], True, True, [("Mtb", pv_), ("Mb", pv_)], [("ps", 0)])
                    for js in J4:
                        self.mm(self.ps[1][:, js], Mb[pv_][:, js], Mtb[pv_][:, js], True, True, [("Mtb", pv_), ("Mb", pv_)], [("ps", 1)])
                    self.cp("act", Mb[cu][:], self.ps[0][:], [("ps", 0)], [("Mb", cu)])
                    yield
                    self.cp("act", Mtb[cu][:], self.ps[1][:], [("ps", 1)], [("Mtb", cu)])
                    yield
                    for js in J4:
                        self.mm(self.ps[0][:, js], Mtb[cu][:, js], X[:, js], True, True, [("Mtb", cu), ("X",)], [("ps", 0)])
                    for js in J4:
                        self.mm(self.ps[1][:, js], Mb[cu][:, js], Tt[:, js], True, True, [("Mb", cu), ("Tt", t2)], [("ps", 1)])
                    yield
                    self.tt("dve", X[:], X[:], self.ps[0][:], ALU.add, [("X",), ("ps", 0)], [("X",)])
                    self.tt("dve", Tt[:], Tt[:], self.ps[1][:], ALU.add, [("Tt", t2), ("ps", 1)], [("Tt", t2)])
                yield
                for js in J4:
                    self.mm(self.ps[0][:, js], Ct1[:, js], X[:, js], True, True, [("Ct1",), ("X",)], [("ps", 0)])
                for js in J4:
                    self.mm(self.ps[1][:, js], C1[:, js], Tt[:, js], True, True, [("C1",), ("Tt", t2)], [("ps", 1)])
                yield
                self.cp("act", Yb[:], self.ps[0][:], [("ps", 0)], [("Yb",)])
                self.cp("act", Ytb[:], self.ps[1][:], [("ps", 1)], [("Ytb",)])
                for js in J4:
                    self.mm(self.ps[0][:, js], Tt[:, js], Yb[:, js], True, True, [("Tt", t2), ("Yb",)], [("ps", 0)])
                for js in J4:
                    self.mm(self.ps[1][:, js], X[:, js], Ytb[:, js], True, True, [("X",), ("Ytb",)], [("ps", 1)])
                yield
                self.tt("dve", X[:], X[:], self.ps[0][:], ALU.subtract, [("X",), ("ps", 0)], [("X",)])
                self.tt("dve", Tt[:], Tt[:], self.ps[1][:], ALU.subtract, [("Tt", t2), ("ps", 1)], [("Tt", t2)])
                yield
                for js in J4:
                    self.mm(self.ps[1][:, js], C2[:, js], Tt[:, js], True, True, [("C2",), ("Tt", t2)], [("ps", 1)])
                yield
                self.cp("act", Ytb[:], self.ps[1][:], [("ps", 1)], [("Ytb",)])
                yield
                for js in J4:
                    self.mm(self.ps[1][:, js], X[:, js], Ytb[:, js], True, True, [("X",), ("Ytb",)], [("ps", 1)])
                self.tt("dve", Tt[:], Tt[:], self.ps[1][:], ALU.subtract, [("Tt", t2), ("ps", 1)], [("Tt", t2)])
                for j in range(4):
                    js = slice(j * 128, (j + 1) * 128)
                    self.mm(self.ps[0][:, js], kbg[:, js], Tt[:, js], True, True, [("kbg", j), ("Tt", t2, j)], [("ps", 0)])
                self.padd("act", lambda e: e.mul(out=nwT[:], in_=self.ps[0][:], mul=-1.0), reads=[("ps", 0)], writes=[("nwT", t2)])
            def recur(tg):
                a = tg * 512
                t2 = tg % 2
                qdecT, kdec, bv, Tt, attnT, nwT, zs = qdecT_[t2], kdec_[t2], bv_[t2], Tt_[t2], attnT_[t2], nwT_[t2], zs_[t2]
                po = self.ps[3]
                for j in range(4):
                    c = tg * 4 + j
                    js = slice(j * 128, (j + 1) * 128)
                    yield
                    sb_cur, sb_nxt = Sbf[c % 2], Sbf[(c + 1) % 2]
                    vn = vnew[c % 2]
                    pv = self.ps[7]
                    self.mm(pv[:, 0:128], Tt[:, js], bv[:, js], True, False, [("Tt", t2, j), ("bv", t2, j)], [("ps", 7)])
                    self.mm(pv[:, 0:128], nwT[:, js], sb_cur[:], False, True, [("nwT", t2), ("Sbf", c % 2)], [("ps", 7)])
                    yield
                    self.cp("act", vn[:], pv[:, 0:128], [("ps", 7)], [("vnew", c % 2)])
                    yield
                    self.mm(po[:, js], sb_cur[:], qdecT[:, js], True, False, [("Sbf", c % 2), ("qdecT", t2)], [("ps", 3)])
                    self.mm(po[:, js], vn[:], attnT[:, js], False, True, [("vnew", c % 2), ("attnT", t2, j)], [("ps", 3)])
                    pS = self.ps[7]
                    self.mm(pS[:, 0:128], kdec[:, js], vn[:], True, True, [("kdec", t2, j), ("vnew", c % 2)], [("ps", 7)])
                    yield
                    self.stt("dve", S[:], S[:], egl[:, c, h:h + 1], pS[:, 0:128], ALU.mult, ALU.add,
                             [("S",), ("egl",), ("ps", 7)], [("S",)])
                    yield
                    self.cp("act", sb_nxt[:], S[:], [("S",)], [("Sbf", (c + 1) % 2)])
                self.cp("act", oTg[:], po[:], [("ps", 3)], [("oTg",)])
                yield
                self.norm_rows(oTg[:], R2o[:], sq_o, rtmp_o, 1.0 / 128.0, self.epsb[:], gdn[:, 0:1], 7, ("oTg",), ("R2o",), kp="dno")
                ob = oab[0]
                self.tt("dve", ob[:], R2o[:], zs[:], ALU.mult, [("R2o",), ("zs", t2)], [("oab", 0)])
                self.dma("sp", self.oa_scr.ap()[h * 128:(h + 1) * 128, a:a + 512], ob[:], [("oab", 0)], [("oa_scr", h, tg)])

            pend = None
            for tg in range(4):
                rp = record_sub(prep(tg))
                if pend is None:
                    emit_ops(rp)
                else:
                    emit_ops(merge_ops(rp, pend))
                pend = record_sub(recur(tg))
            emit_ops(pend)
            yield

    def sb_head(self, es, l, h, hT, wr, gq, gk):
        P, nc = self.P, self.nc
        rg, wring = wr
        if True:
            def A(name, shape, dt):
                k = ("sb", name)
                if k not in self._hc:
                    self._hc[k] = es.enter_context(self.sb(name, shape, dt))
                return self._hc[k]
            kn = A("kn", [128, T], BF16)
            qn = [A("qn%d" % i, [128, 512], BF16) for i in range(2)]
            Vtok = A("Vtok", [128, T], BF16)
            raw = [A("sraw%d" % i, [128, 512], F32) for i in range(1)]
            sq = A("nsq", [128, 512], BF16)
            rtmp = A("nrt", [128, 512], F32)
            e_t = [A("e_t%d" % i, [128, 512], BF16) for i in range(5)]
            spb = [A("spb%d" % i, [128, 512], BF16) for i in range(3)]
            E2 = [A("E2_%d" % i, [128, 512], BF16) for i in range(2)]
            Wt = [A("Wt%d" % i, [128, 512], BF16) for i in range(2)]
            runb = [A("runb%d" % i, [128, 512], BF16) for i in range(2)]
            obb = [A("obb%d" % i, [128, 512], BF16) for i in range(1)]
            def proj_qk(n, tg):
                a = tg * 512
                b = 4
                for c in range(8):
                    self.mm(self.ps[b][:], wring[4 + n][:, c, :], hT[:, c, a:a + 512], c == 0, c == 7,
                            [("wring", rg, 4 + n), ("hT", c, tg)], [("ps", b)])
                r = raw[0]
                self.cp("act", r[:], self.ps[b][:], [("ps", b)], [("sraw", 0)])
                if n == 0:
                    self.norm_rows(r[:], qn[tg % 2][:], sq, rtmp, 1.0 / 128.0, self.epsb[:], gq[:, 0:1], 4, ("sraw", 0), ("qn", tg % 2), kp="sb")
                else:
                    self.norm_rows(r[:], kn[:, a:a + 512], sq, rtmp, 1.0 / 128.0, self.epsb[:], gk[:, 0:1], 4, ("sraw", 0), ("kn", tg), kp="sb")

            for tg in range(4):
                proj_qk(1, tg)
                yield
            for tbg in range(4):
                b = 4 + tbg % 2
                for jj in range(4):
                    tb = tbg * 4 + jj
                    for c in range(8):
                        self.mm(self.ps[b][:, jj * 128:(jj + 1) * 128], hT[:, c, tb * 128:(tb + 1) * 128], wring[6][:, c, :],
                                c == 0, c == 7, [("wring", rg, 6), ("hT", c, tbg)], [("ps", b, jj)])
                self.cp("act", Vtok[:, tbg * 512:(tbg + 1) * 512], self.ps[b][:], [("ps", b)], [("Vtok", tbg)])
                yield
            if self.debug and h == 0:
                self.dbg_out("dbg_kn", kn[:], [128, T], BF16, [("kn",)])
                self.dbg_out("dbg_Vtok", Vtok[:], [128, T], BF16, [("Vtok",)])
            pairs = []
            for qg in range(4):
                nkb = 4 * qg + 4
                for idx, kb in enumerate(range(nkb - 1, -1, -1)):
                    pairs.append((qg, idx, kb, nkb))
            NP = len(pairs)

            def op_mmz(p):
                qg, idx, kb, nkb = pairs[p]
                if idx == 0:
                    proj_qk(0, qg)
                ks = slice(kb * 128, (kb + 1) * 128)
                self.mm(self.ps[4][:], kn[:, ks], qn[qg % 2][:], True, True, [("kn", kb // 4), ("qn", qg % 2)], [("ps", 4)])

            def op_exp(p):
                qg, idx, kb, nkb = pairs[p]
                e5 = p % 5
                self.actf(e_t[e5][:], self.ps[4][:], AF.Exp, [("ps", 4)], [("e_t", e5)])
                r = kb - 4 * qg
                if r >= 0:
                    self.tt("dve", e_t[e5][:], e_t[e5][:], self.mask01[r][:], ALU.mult, [("e_t", e5), ("mask01", r)], [("e_t", e5)])

            def op_ln(p):
                e5, e3 = p % 5, p % 3
                self.actf(spb[e3][:], e_t[e5][:], AF.Ln, [("e_t", e5), ("one1",)], [("spb", e3)], bias=self.one1[:])

            def op_smm(p):
                qg, idx, kb, nkb = pairs[p]
                e3 = p % 3
                pS = self.ps[5]
                self.mm(pS[:], self.triI[:], spb[e3][:], True, idx == 0, [("triI",), ("spb", e3)], [("ps", 5)])
                if idx >= 1:
                    self.mm(pS[:], self.ones[:], runb[(idx - 1) % 2][:], False, True, [("ones",), ("runb", (idx - 1) % 2)], [("ps", 5)])
                if kb > 0:
                    if idx == 0:
                        self.cp("dve", runb[0][:], spb[e3][:], [("spb", e3)], [("runb", 0)])
                    else:
                        self.tt("dve", runb[idx % 2][:], runb[(idx - 1) % 2][:], spb[e3][:], ALU.add,
                                [("runb", (idx - 1) % 2), ("spb", e3)], [("runb", idx % 2)])

            def op_e2(p):
                self.actf(E2[p % 2][:], self.ps[5][:], AF.Exp, [("ps", 5)], [("E2", p % 2)], scale=-1.0)

            def op_w(p):
                e5 = p % 5
                self.tt("dve", Wt[p % 2][:], e_t[e5][:], E2[p % 2][:], ALU.mult, [("e_t", e5), ("E2", p % 2)], [("Wt", p % 2)])

            def op_pv(p):
                qg, idx, kb, nkb = pairs[p]
                qa = qg * 512
                ks = slice(kb * 128, (kb + 1) * 128)
                po = self.ps[6]
                self.mm(po[:], Vtok[:, ks], Wt[p % 2][:], idx == 0, idx == nkb - 1, [("Vtok", kb // 4), ("Wt", p % 2)], [("ps", 6)])
                if idx == nkb - 1:
                    ob = obb[0]
                    self.cp("act", ob[:], po[:], [("ps", 6)], [("obb", 0)])
                    self.dma("sp", self.ob_scr.ap()[h * 128:(h + 1) * 128, qa:qa + 512], ob[:], [("obb", 0)], [("ob_scr", h, qg)])

            stages = [(op_pv, 6), (op_w, 5), (op_e2, 4), (op_smm, 3), (op_ln, 2), (op_exp, 1), (op_mmz, 0)]
            for t in range(NP + 6):
                for fn, d in stages:
                    p = t - d
                    if 0 <= p < NP:
                        fn(p)
                yield

    def mix_out(self, l, hT):
        P, nc = self.P, self.nc
        wa_v = self.w["w_branch_a"].ap()[l].rearrange("(c p) n -> p c n", p=128)
        wb_v = self.w["w_branch_b"].ap()[l].rearrange("(c p) n -> p c n", p=128)
        wo_v = self.w["w_out"].ap()[l].rearrange("(c p) n -> p c n", p=128)
        w_in_v = self.w["w_in"].ap()[l].rearrange("(c p) n -> p c n", p=128)
        oa_v = self.oa_scr.ap().rearrange("(c p) t -> p c t", p=128)
        ob_v = self.ob_scr.ap().rearrange("(c p) t -> p c t", p=128)
        with ExitStack() as es:
            A = lambda name, shape, dt: es.enter_context(self.sb(name, shape, dt))
            NW = 10
            wr = [A("owr%d" % i, [128, 8, 128], BF16) for i in range(NW)]
            oat = [A("oat%d" % i, [128, 8, 512], BF16) for i in range(2)]
            obt = [A("obt%d" % i, [128, 8, 512], BF16) for i in range(2)]
            merged = A("merged", [128, 8, 512], BF16)
            sa = [A("sga%d" % i, [128, 512], F32) for i in range(2)]
            sb = [A("sgb%d" % i, [128, 512], F32) for i in range(2)]
            m1 = [A("m1_%d" % i, [128, 512], F32) for i in range(2)]
            m2 = [A("m2_%d" % i, [128, 512], F32) for i in range(2)]
            sched = []
            for tg in range(4):
                for c in range(8):
                    sched.append((wa_v, c * 128))
                    sched.append((wb_v, c * 128))
                    sched.append((w_in_v, 7184 + c * 128))
                    sched.append((w_in_v, 8208 + c * 128))
                for c2 in range(8):
                    sched.append((wo_v, c2 * 128))
            nload = [0]

            def issue():
                i = nload[0]
                if i < len(sched):
                    src, col = sched[i]
                    self.dma("pool", wr[i % NW][:], src[:, :, col:col + 128], [], [("owr", i % NW)])
                    nload[0] += 1

            for _ in range(NW - 2):
                issue()
            wi = 0
            for tg in range(4):
                a = tg * 512
                self.dma("sp", oat[tg % 2][:], oa_v[:, :, a:a + 512], [("oa_scr",)], [("oat", tg % 2)])
                self.dma("sp", obt[tg % 2][:], ob_v[:, :, a:a + 512], [("ob_scr",)], [("obt", tg % 2)])
                for c in range(8):
                    s4 = c % 2
                    banks = [0 + 4 * s4, 1 + 4 * s4, 2 + 4 * s4, 3 + 4 * s4]
                    srcs = [(oat[tg % 2], ("oat", tg % 2)), (obt[tg % 2], ("obt", tg % 2)), None, None]
                    for q in range(4):
                        slot = wi % NW
                        wi += 1
                        issue()
                        for k in range(8):
                            if q < 2:
                                rhs, rk = srcs[q][0][:, k, :], srcs[q][1]
                            else:
                                rhs, rk = hT[:, k, a:a + 512], ("hT", k, tg)
                            self.mm(self.ps[banks[q]][:], wr[slot][:, k, :], rhs, k == 0, k == 7, [("owr", slot), rk], [("ps", banks[q])])
                    self.actf(sa[s4][:], self.ps[banks[2]][:], AF.Sigmoid, [("ps", banks[2])], [("sga", s4)])
                    self.actf(sb[s4][:], self.ps[banks[3]][:], AF.Sigmoid, [("ps", banks[3])], [("sgb", s4)])
                    self.tt("dve", m1[s4][:], sa[s4][:], self.ps[banks[0]][:], ALU.mult, [("sga", s4), ("ps", banks[0])], [("m1", s4)])
                    self.tt("dve", m2[s4][:], sb[s4][:], self.ps[banks[1]][:], ALU.mult, [("sgb", s4), ("ps", banks[1])], [("m2", s4)])
                    self.tt("pool", merged[:, c, :], m1[s4][:], m2[s4][:], ALU.add, [("m1", s4), ("m2", s4)], [("merged", c)])
                for c2 in range(8):
                    slot = wi % NW
                    wi += 1
                    issue()
                    b = c2 % 8
                    for k in range(8):
                        self.mm(self.ps[b][:], wr[slot][:, k, :], merged[:, k, :], k == 0, k == 7, [("owr", slot), ("merged", k)], [("ps", b)])
                    xs = self.xT[:, c2, a:a + 512]
                    self.tt("dve", xs, xs, self.ps[b][:], ALU.add,
                            [("ps", b)] + [("xT", c2, a // 128 + kk) for kk in range(4)],
                            [("xT", c2, a // 128 + kk) for kk in range(4)])
            P.flush()

    def dbg_out(self, name, ap, shape, dt, R):
        t = self.nc.dram_tensor(name, list(shape), dt, kind="ExternalOutput")
        self.dbg_names.append(name)
        self.dma("sp", t.ap(), ap, R, [("dbg", name)])


_CACHE = {}


def _get_nc(layers, stages, heads=8, debug=False):
    key = (layers, tuple(stages), heads, debug)
    if key not in _CACHE:
        b = Builder(layers=layers, stages=stages, heads=heads, debug=debug)
        _CACHE[key] = (b.build(), b)
    return _CACHE[key][0]


def run(inputs, layers=DEPTH, stages=("ffn1", "mix", "ffn2"), trace=False, heads=8, debug=False):
    nc = _get_nc(layers, stages, heads, debug)
    x = np.ascontiguousarray(inputs["x"], dtype=np.float32)
    B = x.shape[0]
    in_maps = []
    for b in range(B):
        m = {"x": x[b]}
        for k, v in inputs.items():
            if k != "x":
                m[k] = np.ascontiguousarray(v, dtype=np.float32)
        in_maps.append(m)
    res = run_bass_kernel_spmd(nc, in_maps, core_ids=list(range(B)), trace=trace)
    out = np.stack([np.asarray(r["out"]) for r in res.results], axis=0)
    return out, res


def kernel(**inputs):
    out, _ = run(inputs)
    return out.astype(np.float32)
```

```python
import numpy as np
from contextlib import ExitStack
import concourse.bass as bass
import concourse.mybir as mybir
from concourse.bass_utils import run_bass_kernel_spmd

F32 = mybir.dt.float32
BF16 = mybir.dt.bfloat16
ALU = mybir.AluOpType
AF = mybir.ActivationFunctionType

D = 1024
T = 2048
DEPTH = 4
DFF = 2816
NFF = DFF // 128
NIN = 9232
EPS = 1e-6


class _Node:
    __slots__ = ("w", "r", "ch")

    def __init__(self):
        self.w = None
        self.r = {}
        self.ch = {}


class Op:
    __slots__ = ("id", "eng", "fn", "deps", "dma", "lane", "sig", "val", "stream")


class Prog:
    COMPUTE = ("pe", "dve", "act", "pool")
    ENGS = ("pe", "dve", "act", "pool", "sp")

    def __init__(self, nc, es, n_lanes=24):
        self.nc = nc
        self.ops = []
        self.root = _Node()
        self.n_lanes = n_lanes
        self.lane_last = [None] * n_lanes
        self.lane_cnt = [0] * n_lanes
        self.n_dma = 0
        self.n_dma_sw = 0
        self.flushed = 0
        self.sem = {e: es.enter_context(nc.semaphore("s_" + e)) for e in self.COMPUTE}
        self.lsem = [es.enter_context(nc.semaphore("l_%d" % i)) for i in range(n_lanes)]
        self.cnt = {e: 0 for e in self.COMPUTE}
        self.seen = {e: {} for e in self.ENGS}
        self.last_stream = {}
        self.nblk = 0

    def _walk(self, key):
        node = self.root
        anc = []
        for k in key:
            nxt = node.ch.get(k)
            if nxt is None:
                nxt = _Node()
                node.ch[k] = nxt
            node = nxt
            anc.append(node)
        me = anc[-1]
        desc = []
        stack = list(me.ch.values())
        while stack:
            n = stack.pop()
            desc.append(n)
            stack.extend(n.ch.values())
        return anc[:-1], me, desc

    def add(self, eng, fn, reads=(), writes=(), dma=False):
        reads = [k[:2] if k[0] == "ps" else k for k in reads]
        writes = [k[:2] if k[0] == "ps" else k for k in writes]
        op = Op()
        op.id = len(self.ops)
        op.eng = eng
        op.fn = fn
        op.dma = dma
        op.sig = dma
        op.val = None
        op.lane = None
        deps = {}

        def dep(d, kind):
            if d is None:
                return
            deps.setdefault(d, set()).add(kind)

        if dma:
            half = self.n_lanes // 2
            if eng == "pool":
                op.lane = half + self.n_dma_sw % half
                self.n_dma_sw += 1
            else:
                op.lane = self.n_dma % half
                self.n_dma += 1
            dep(self.lane_last[op.lane], "LANE")
            self.lane_last[op.lane] = op.id
            self.lane_cnt[op.lane] += 1
            op.val = 16 * self.lane_cnt[op.lane]
            op.stream = ("lane", op.lane)
        else:
            op.stream = eng
        for key in reads:
            anc, me, desc = self._walk(key)
            for n in anc + [me] + desc:
                dep(n.w, "RAW")
                if key[0] == "ps":
                    for st, d in n.r.items():
                        if st != op.stream:
                            dep(d, "RAR")
        for key in writes:
            anc, me, desc = self._walk(key)
            for n in anc + [me] + desc:
                dep(n.w, "WAW")
                for d in n.r.values():
                    dep(d, "WAR")
        for key in reads:
            anc, me, desc = self._walk(key)
            me.r[op.stream] = op.id
        for key in writes:
            anc, me, desc = self._walk(key)
            me.w = op.id
            me.r = {}
            for n in desc:
                n.w = None
                n.r = {}
        deps.pop(op.id, None)
        op.deps = deps
        self.ops.append(op)
        if fn is not None:
            self.last_stream[op.stream] = op.id
        return op

    def barrier(self):
        last = dict(self.last_stream)
        for e in self.ENGS:
            op = self.add(e, None)
            for s, d in last.items():
                if d != op.id:
                    op.deps.setdefault(d, set()).add("BAR")

    def _needs_wait(self, op, d, kinds):
        if d < self.flushed:
            return False
        dop = self.ops[d]
        if dop.dma or op.dma:
            return True
        if dop.eng != op.eng:
            return True
        if op.eng == "pe":
            return False
        if op.eng == "pool":
            return True
        return True

    def flush(self):
        self.barrier()
        ops = self.ops[self.flushed:]
        if not ops:
            return
        for op in ops:
            for d, kinds in op.deps.items():
                if self._needs_wait(op, d, kinds):
                    dop = self.ops[d]
                    if not dop.dma and dop.val is None and dop.id >= self.flushed:
                        dop.sig = True
                    elif not dop.dma and dop.val is None:
                        raise RuntimeError("dep on unsignalled flushed op")
        for op in ops:
            if not op.dma and op.sig and op.fn is not None:
                self.cnt[op.eng] += 1
                op.val = self.cnt[op.eng]
        nc = self.nc
        per = {e: [o for o in ops if o.eng == e] for e in self.ENGS}
        self.nblk += 1
        with nc.Block() as block:
            def emit(engname, eng):
                seen = self.seen[engname]
                for op in per[engname]:
                    for d, kinds in op.deps.items():
                        if not self._needs_wait(op, d, kinds):
                            continue
                        dop = self.ops[d]
                        if dop.dma:
                            sem, key = self.lsem[dop.lane], ("l", dop.lane)
                        else:
                            sem, key = self.sem[dop.eng], dop.eng
                        if dop.val is None:
                            raise RuntimeError("dep without value: op %d -> %d" % (op.id, d))
                        if seen.get(key, 0) >= dop.val:
                            continue
                        seen[key] = dop.val
                        eng.wait_ge(sem, dop.val)
                    if op.fn is None:
                        continue
                    ins = op.fn(eng)
                    if op.dma:
                        ins.then_inc(self.lsem[op.lane], 16)
                    elif op.sig:
                        ins.then_inc(self.sem[op.eng], 1)

            if per["pe"]:
                @block.tensor
                def _(e):
                    emit("pe", e)
            if per["dve"]:
                @block.vector
                def _(e):
                    emit("dve", e)
            if per["act"]:
                @block.scalar
                def _(e):
                    emit("act", e)
            if per["pool"]:
                @block.gpsimd
                def _(e):
                    emit("pool", e)
            if per["sp"]:
                @block.sync
                def _(e):
                    emit("sp", e)
        self.flushed = len(self.ops)

    def finish(self):
        self.flush()


def merge_ops(a, b):
    na, nb = len(a), len(b)
    out = []
    i = j = 0
    while i < na or j < nb:
        if j >= nb or (i < na and i * max(nb, 1) <= j * max(na, 1)):
            out.append(a[i])
            i += 1
        else:
            out.append(b[j])
            j += 1
    return out


def merge_n(lists):
    pos = [0] * len(lists)
    out = []
    total = sum(len(x) for x in lists)
    while len(out) < total:
        best, bf = None, None
        for k, lst in enumerate(lists):
            if pos[k] < len(lst):
                f = pos[k] / float(len(lst))
                if bf is None or f < bf:
                    best, bf = k, f
        out.append(lists[best][pos[best]])
        pos[best] += 1
    return out


class XAlias:
    def __init__(self, xT):
        self.f = xT
        self.b = xT.bitcast(BF16)
        self.reset()

    def reset(self):
        self.row = 0
        self.off = 0

    def alloc(self, n, dt, row=None, off=None):
        nbytes = n * (4 if dt == F32 else 2)
        if row is None:
            if self.off + nbytes > 8192:
                self.row += 1
                self.off = 0
            assert self.row < 6, "alias rows exhausted"
            row, off = self.row, self.off
            self.off += nbytes
        if dt == F32:
            return self.f[:, row, off // 4:off // 4 + n]
        return self.b[:, row, off // 2:off // 2 + n]


class Builder:
    def __init__(self, layers=DEPTH, stages=("ffn1", "mix", "ffn2"), heads=8, debug=False, cut=""):
        self.cut = cut
        self.dn_steps = 3
        self.layers = layers
        self.stages = stages
        self.heads = heads
        self.debug = debug
        self.dbg_names = []
        self._uid = 0
        self._rec = None
        self.nc = bass.Bass("TRN2", target_bir_lowering=False)

    def padd(self, *a, **kw):
        if self._rec is not None:
            self._rec.append((a, kw))
        else:
            self.P.add(*a, **kw)

    def sb(self, name, shape, dt):
        self._uid += 1
        return self.nc.sbuf_tensor("%s_u%d" % (name, self._uid), shape, dt)

    def dram_in(self, name, shape):
        return self.nc.dram_tensor(name, list(shape), F32, kind="ExternalInput")

    def build(self):
        nc = self.nc
        L = DEPTH
        self.x_in = self.dram_in("x", [T, D])
        self.w = {}
        for name, shape in [
            ("ffn1_norm", [L, D]), ("ffn1_w_in", [L, D, 2 * DFF]), ("ffn1_w_out", [L, DFF, D]),
            ("mix_norm", [L, D]), ("w_in", [L, D, NIN]), ("dn_conv_w", [L, 4, 3 * D]),
            ("dn_a_log", [L, 8]), ("dn_dt_bias", [L, 8]), ("dn_out_norm", [L, 128]),
            ("sb_q_norm", [L, 128]), ("sb_k_norm", [L, 128]),
            ("w_branch_a", [L, D, D]), ("w_branch_b", [L, D, D]), ("w_out", [L, D, D]),
            ("ffn2_norm", [L, D]), ("ffn2_w_in", [L, D, 2 * DFF]), ("ffn2_w_out", [L, DFF, D]),
        ]:
            self.w[name] = self.dram_in(name, shape)
        self.out = nc.dram_tensor("out", [T, D], F32, kind="ExternalOutput")
        skind = "ExternalOutput" if self.debug else "Internal"
        self.oa_scr = nc.dram_tensor("oa_scr", [D, T], BF16, kind=skind)
        self.ob_scr = nc.dram_tensor("ob_scr", [D, T], BF16, kind=skind)
        self.x_scr = nc.dram_tensor("x_scr", [D, T], F32, kind="Internal")

        with ExitStack() as es:
            P = self.P = Prog(nc, es)
            self.es = es
            self.xT = es.enter_context(self.sb("xT", [128, 8, T], F32))
            self.ones = es.enter_context(self.sb("ones", [128, 128], BF16))
            self.ident = es.enter_context(self.sb("ident", [128, 128], F32))
            self.gains = es.enter_context(self.sb("gains", [128, 3, L, 8], F32))
            self.ps = [es.enter_context(nc.psum_tensor("ps%d" % i, [128, 512], F32)) for i in range(8)]
            self.epsb = es.enter_context(self.sb("epsb", [128, 1], F32))
            self.setup_consts()
            if "mix" in self.stages:
                self.setup_mixer_consts(es)
            self.load_x()
            P.flush()
            for l in range(self.layers):
                if "ffn1" in self.stages:
                    self.ffn(l, 0)
                if "mix" in self.stages:
                    self.mixer(l)
                if "ffn2" in self.stages:
                    self.ffn(l, 2)
            self.store_x()
            P.finish()
        return nc

    def setup_consts(self):
        P, nc = self.P, self.nc
        ones, ident, gains = self.ones, self.ident, self.gains
        P.add("pool", lambda e: e.memset(ones[:], 1.0), writes=[("ones",)])
        P.add("pool", lambda e: e.memset(self.epsb[:], EPS), writes=[("epsb",)])
        P.add("pool", lambda e: e.memset(ident[:], 1.0), writes=[("ident",)])
        P.add("pool", lambda e: e.affine_select(
            out=ident[:], in_=ident[:], pattern=[[-1, 128]], compare_op=ALU.is_equal,
            fill=0.0, base=0, channel_multiplier=1), reads=[("ident",)], writes=[("ident",)])
        for i, nm in enumerate(("ffn1_norm", "mix_norm", "ffn2_norm")):
            src = self.w[nm].ap().rearrange("l (c p) -> p l c", p=128)
            with nc.allow_non_contiguous_dma(reason="tiny gain load"):
                pass
            P.add("sp", (lambda e, i=i, src=src: e.dma_start(
                out=gains[:, i, :, :], in_=src, allow_slow_non_contiguous=True)),
                writes=[("gains", i)], dma=True)

    def load_x(self):
        P = self.P
        with ExitStack() as es:
            nc = self.nc
            xin = [es.enter_context(self.sb("xin%d" % i, [128, D], F32)) for i in range(3)]
            for tb in range(T // 128):
                b = xin[tb % 3]
                P.add("sp", (lambda e, b=b, tb=tb: e.dma_start(out=b[:], in_=self.x_in.ap()[tb * 128:(tb + 1) * 128, :])),
                      writes=[("xin", tb % 3)], dma=True)
                for g in range(2):
                    for cc in range(4):
                        c = g * 4 + cc
                        pst = self.ps[(tb * 2 + g) % 8]
                        P.add("pe", (lambda e, pst=pst, b=b, c=c, cc=cc: e.transpose(
                            out=pst[:, cc * 128:(cc + 1) * 128], in_=b[:, c * 128:(c + 1) * 128], identity=self.ident[:])),
                            reads=[("xin", tb % 3), ("ident",)], writes=[("ps", (tb * 2 + g) % 8, cc)])
                    pst = self.ps[(tb * 2 + g) % 8]
                    eng = "dve" if g == 0 else "act"
                    dst = self.xT[:, g * 4:(g + 1) * 4, tb * 128:(tb + 1) * 128]
                    srcv = pst[:].rearrange("p (c t) -> p c t", c=4)
                    if eng == "dve":
                        P.add("dve", (lambda e, dst=dst, srcv=srcv: e.tensor_copy(out=dst, in_=srcv)),
                              reads=[("ps", (tb * 2 + g) % 8)], writes=[("xT", g * 4 + k, tb) for k in range(4)])
                    else:
                        P.add("act", (lambda e, dst=dst, srcv=srcv: e.copy(out=dst, in_=srcv)),
                              reads=[("ps", (tb * 2 + g) % 8)], writes=[("xT", g * 4 + k, tb) for k in range(4)])
            P.flush()

    def store_x(self):
        P = self.P
        with ExitStack() as es:
            nc = self.nc
            xo = [es.enter_context(self.sb("xo%d" % i, [128, D], F32)) for i in range(3)]
            for tb in range(T // 128):
                b = xo[tb % 3]
                for g in range(2):
                    for cc in range(4):
                        c = g * 4 + cc
                        pst = self.ps[(tb * 2 + g) % 8]
                        P.add("pe", (lambda e, pst=pst, c=c, cc=cc, tb=tb: e.transpose(
                            out=pst[:, cc * 128:(cc + 1) * 128], in_=self.xT[:, c, tb * 128:(tb + 1) * 128],
                            identity=self.ident[:])),
                            reads=[("xT",), ("ident",)], writes=[("ps", (tb * 2 + g) % 8, cc)])
                    pst = self.ps[(tb * 2 + g) % 8]
                    dst = b[:, g * 512:(g + 1) * 512]
                    if g == 0:
                        P.add("dve", (lambda e, dst=dst, pst=pst: e.tensor_copy(out=dst, in_=pst[:])),
                              reads=[("ps", (tb * 2 + g) % 8)], writes=[("xo", tb % 3, g)])
                    else:
                        P.add("act", (lambda e, dst=dst, pst=pst: e.copy(out=dst, in_=pst[:])),
                              reads=[("ps", (tb * 2 + g) % 8)], writes=[("xo", tb % 3, g)])
                P.add("sp", (lambda e, b=b, tb=tb: e.dma_start(out=self.out.ap()[tb * 128:(tb + 1) * 128, :], in_=b[:])),
                      reads=[("xo", tb % 3)], dma=True)
            P.flush()

    def rmsnorm_T(self, hT, which, l, t0, nt, es_tmp):
        P, nc = self.P, self.nc
        sq, rstd = es_tmp
        for tg in range(nt // 512):
            a = t0 + tg * 512
            pb = self.ps[(tg % 2)]
            for c in range(8):
                s = sq[c % 2]
                P.add("act", (lambda e, s=s, c=c, a=a: e.activation(
                    out=s[:], in_=self.xT[:, c, a:a + 512], func=AF.Square)),
                    reads=[("xT", c, a // 128 + k) for k in range(4)],
                    writes=[("sq", c % 2)])
                P.add("pe", (lambda e, pb=pb, s=s, c=c: e.matmul(pb[:], lhsT=self.ones[:], rhs=s[:], start=(c == 0), stop=(c == 7))),
                      reads=[("sq", c % 2), ("ones",)], writes=[("ps", tg % 2)])
            r = rstd[tg % 2]
            P.add("act", (lambda e, r=r, pb=pb: e.activation(
                out=r[:], in_=pb[:], func=AF.Ln, bias=self.epsb[:], scale=1.0 / D)),
                reads=[("ps", tg % 2), ("epsb",)], writes=[("rstd", tg % 2)])
            P.add("act", (lambda e, r=r: e.activation(out=r[:], in_=r[:], func=AF.Exp, scale=-0.5)),
                reads=[("rstd", tg % 2)], writes=[("rstd", tg % 2)])
            for c in range(8):
                P.add("dve", (lambda e, r=r, c=c, a=a, tg=tg: e.scalar_tensor_tensor(
                    out=hT[:, c, tg * 512:(tg + 1) * 512], in0=self.xT[:, c, a:a + 512],
                    scalar=self.gains[:, which, l, c:c + 1], in1=r[:], op0=ALU.mult, op1=ALU.mult)),
                    reads=[("xT", c, a // 128 + k) for k in range(4)] + [("rstd", tg % 2), ("gains", which)],
                    writes=[("hT", c, tg)])

    def ffn(self, l, which):
        P, nc = self.P, self.nc
        pre = "ffn1" if which == 0 else "ffn2"
        w_in = self.w[pre + "_w_in"].ap()[l]
        w_out = self.w[pre + "_w_out"].ap()[l]
        HT = 1024
        G = 4
        groups = [list(range(s, min(s + G, NFF))) for s in range(0, NFF, G)]
        with ExitStack() as es:
            hT = es.enter_context(self.sb("hT", [128, 8, HT], BF16))
            sq = [es.enter_context(self.sb("sq%d" % i, [128, 512], BF16)) for i in range(2)]
            rstd = [es.enter_context(self.sb("rstd%d" % i, [128, 512], F32)) for i in range(2)]
            NW = 3
            wi = [es.enter_context(self.sb("wi%d" % i, [128, 2, 8, 128], BF16)) for i in range(NW)]
            wo = [es.enter_context(self.sb("wo%d" % i, [128, D], BF16)) for i in range(2 * G)]
            actT = [es.enter_context(self.sb("actT%d" % i, [128, HT], BF16)) for i in range(2 * G)]
            sg = [es.enter_context(self.sb("sg%d" % i, [128, 512], F32)) for i in range(2)]
            w_in_v = w_in.rearrange("(c p) (u f) -> p u c f", p=128, u=2)
            w_out_v = w_out.rearrange("(j p) d -> p j d", p=128)

            def load_wi(j, slot):
                for u in range(2):
                    P.add("pool", (lambda e, u=u: e.dma_start(out=wi[slot][:, u, :, :], in_=w_in_v[:, u, :, j * 128:(j + 1) * 128])),
                          writes=[("wi", slot, u)], dma=True)

            def load_wo(j, slot):
                P.add("pool", (lambda e: e.dma_start(out=wo[slot][:], in_=w_out_v[:, j, :])),
                      writes=[("wo", slot)], dma=True)

            for half in range(T // HT):
                t0 = half * HT
                self.rmsnorm_T(hT, which, l, t0, HT, (sq, rstd))
                seq = [(half, j) for j in range(NFF)]
                load_wi(0, 0)
                load_wi(1, 1)
                nsg = 0
                for gi, grp in enumerate(groups):
                    gb = (gi % 2) * G
                    for jj, j in enumerate(grp):
                        slot = j % NW
                        if j + 2 < NFF:
                            load_wi(j + 2, (j + 2) % NW)
                        load_wo(j, gb + jj)
                        for u in range(2):
                            for tg in range(2):
                                bank = 2 + u * 2 + tg
                                for c in range(8):
                                    P.add("pe", (lambda e, bank=bank, slot=slot, c=c, u=u, tg=tg: e.matmul(
                                        self.ps[bank][:], lhsT=wi[slot][:, u, c, :], rhs=hT[:, c, tg * 512:(tg + 1) * 512],
                                        start=(c == 0), stop=(c == 7))),
                                        reads=[("wi", slot, u), ("hT", c, tg)], writes=[("ps", bank)])
                        for tg in range(2):
                            s = sg[nsg % 2]
                            P.add("act", (lambda e, s=s, tg=tg: e.activation(out=s[:], in_=self.ps[2 + tg][:], func=AF.Silu)),
                                  reads=[("ps", 2 + tg)], writes=[("sg", nsg % 2)])
                            P.add("dve", (lambda e, s=s, tg=tg, gb=gb, jj=jj: e.tensor_tensor(
                                out=actT[gb + jj][:, tg * 512:(tg + 1) * 512], in0=s[:], in1=self.ps[4 + tg][:], op=ALU.mult)),
                                reads=[("sg", nsg % 2), ("ps", 4 + tg)], writes=[("actT", gb + jj, tg)])
                            nsg += 1
                    for c in range(8):
                        for tg in range(2):
                            bank = 6 + (c * 2 + tg) % 2
                            for jj, j in enumerate(grp):
                                P.add("pe", (lambda e, bank=bank, gb=gb, jj=jj, c=c, tg=tg, n=len(grp): e.matmul(
                                    self.ps[bank][:], lhsT=wo[gb + jj][:, c * 128:(c + 1) * 128],
                                    rhs=actT[gb + jj][:, tg * 512:(tg + 1) * 512], start=(jj == 0), stop=(jj == n - 1))),
                                    reads=[("wo", gb + jj), ("actT", gb + jj, tg)], writes=[("ps", bank)])
                            a = t0 + tg * 512
                            xs = self.xT[:, c, a:a + 512]
                            P.add("dve", (lambda e, xs=xs, bank=bank: e.scalar_tensor_tensor(
                                out=xs, in0=self.ps[bank][:], scalar=0.5, in1=xs, op0=ALU.mult, op1=ALU.add)),
                                reads=[("ps", bank)] + [("xT", c, a // 128 + k) for k in range(4)],
                                writes=[("xT", c, a // 128 + k) for k in range(4)])
                            if which == 0 and "mix" in self.stages and gi == len(groups) - 1:
                                self.dma("sp", self.x_scr.ap()[c * 128:(c + 1) * 128, a:a + 512], xs,
                                         [("xT", c, a // 128 + k) for k in range(4)], [("x_scr", c, a // 512)])
            P.flush()

    def mm(self, out, lhsT, rhs, start, stop, R, W):
        self.padd("pe", lambda e: e.matmul(out, lhsT=lhsT, rhs=rhs, start=start, stop=stop), reads=R, writes=W)

    def tr(self, out, in_, ident, R, W):
        self.padd("pe", lambda e: e.transpose(out=out, in_=in_, identity=ident), reads=R, writes=W)

    def actf(self, out, in_, func, R, W, bias=None, scale=None):
        kw = {}
        if bias is not None:
            kw["bias"] = bias
        if scale is not None:
            kw["scale"] = scale
        self.padd("act", lambda e: e.activation(out=out, in_=in_, func=func, **kw), reads=R, writes=W)

    def tt(self, eng, out, in0, in1, op, R, W):
        self.padd(eng, lambda e: e.tensor_tensor(out=out, in0=in0, in1=in1, op=op), reads=R, writes=W)

    def ts(self, eng, out, in0, s1, s2, op0, op1, R, W):
        if s2 is None:
            self.padd(eng, lambda e: e.tensor_scalar(out=out, in0=in0, scalar1=s1, scalar2=None, op0=op0), reads=R, writes=W)
        else:
            self.padd(eng, lambda e: e.tensor_scalar(out=out, in0=in0, scalar1=s1, scalar2=s2, op0=op0, op1=op1), reads=R, writes=W)

    def stt(self, eng, out, in0, scalar, in1, op0, op1, R, W):
        self.padd(eng, lambda e: e.scalar_tensor_tensor(out=out, in0=in0, scalar=scalar, in1=in1, op0=op0, op1=op1),
                   reads=R, writes=W)

    def cp(self, eng, out, in_, R, W):
        if eng == "act":
            self.padd("act", lambda e: e.copy(out=out, in_=in_), reads=R, writes=W)
        else:
            self.padd(eng, lambda e: e.tensor_copy(out=out, in_=in_), reads=R, writes=W)

    def dma(self, eng, out, in_, R, W, slow=False):
        if slow:
            self.padd(eng, lambda e: e.dma_start(out=out, in_=in_, allow_slow_non_contiguous=True), reads=R, writes=W, dma=True)
        else:
            self.padd(eng, lambda e: e.dma_start(out=out, in_=in_), reads=R, writes=W, dma=True)

    def sel(self, out, pattern, cmp, fill, base, cm, key):
        self.padd("pool", lambda e: e.affine_select(out=out, in_=out, pattern=pattern, compare_op=cmp, fill=fill,
                                                     base=base, channel_multiplier=cm), reads=[key], writes=[key])

    def memset(self, ap, val, key):
        self.padd("pool", lambda e: e.memset(ap, val), writes=[key])

    def setup_mixer_consts(self, es):
        nc, P = self.nc, self.P
        A = lambda name, shape, dt: es.enter_context(self.sb(name, shape, dt))
        NEG = -30000.0
        self.triU = A("triU", [128, 128], F32)
        self.sel127 = A("sel127", [128, 128], F32)
        self.triI = A("triI", [128, 128], BF16)
        self.SU = A("SU", [128, 128], F32)
        self.NEGlo = A("NEGlo", [128, 128], F32)
        self.NEGui = A("NEGui", [128, 128], F32)
        self.cmask = A("cmask", [128, 512], F32)
        self.mask01 = [A("mask01_%d" % r, [128, 512], BF16) for r in range(4)]
        self.identb = A("identb", [128, 128], BF16)
        self.eps128 = A("eps128", [128, 1], F32)
        self.one1 = A("one1", [128, 1], F32)
        self.memset(self.triU[:], 1.0, ("triU",))
        self.sel(self.triU[:], [[1, 128]], ALU.is_ge, 0.0, 0, -1, ("triU",))
        self.memset(self.sel127[:], 1.0, ("sel127",))
        self.sel(self.sel127[:], [[0, 128]], ALU.is_equal, 0.0, -127, 1, ("sel127",))
        self.memset(self.SU[:], 1.0, ("SU",))
        self.sel(self.SU[:], [[1, 128]], ALU.is_gt, 0.0, 0, -1, ("SU",))
        self.memset(self.NEGlo[:], 0.0, ("NEGlo",))
        self.sel(self.NEGlo[:], [[1, 128]], ALU.is_ge, NEG, 0, -1, ("NEGlo",))
        self.memset(self.NEGui[:], 0.0, ("NEGui",))
        self.sel(self.NEGui[:], [[-1, 128]], ALU.is_gt, NEG, 0, 1, ("NEGui",))
        self.memset(self.cmask[:], 1.0, ("cmask",))
        self.memset(self.cmask[:].rearrange("p (c j) -> p c j", j=128)[:, :, 0:1], 0.0, ("cmask",))
        self.mA = A("mA", [128, 128], BF16)
        self.mB = A("mB", [128, 128], BF16)
        self.mC = A("mC", [128, 128], BF16)
        for (m, bs, key) in ((self.mA, 32, ("mA",)), (self.mC, 64, ("mC",))):
            for b in range(128 // bs):
                cs = m[:, b * bs:(b + 1) * bs]
                self.memset(cs, 1.0, key)
                self.sel(cs, [[0, bs]], ALU.is_ge, 0.0, -bs * b, 1, key)
                self.sel(cs, [[0, bs]], ALU.is_ge, 0.0, bs * b + bs - 1, -1, key)
        self.tt("pool", self.mB[:], self.mC[:], self.mA[:], ALU.subtract, [("mA",), ("mC",)], [("mB",)])
        self.ts("pool", self.mC[:], self.mC[:], -1.0, 1.0, ALU.mult, ALU.add, [("mC",)], [("mC",)])
        self.memset(self.eps128[:], 128.0 * EPS, ("eps128",))
        self.memset(self.one1[:], 1.0, ("one1",))
        self.cp("pool", self.identb[:], self.ident[:], [("ident",)], [("identb",)])
        with ExitStack() as es2:
            tmp = es2.enter_context(self.sb("ctmp", [128, 512], F32))
            self.memset(tmp[:, 0:128], 1.0, ("ctmp",))
            self.sel(tmp[:, 0:128], [[-1, 128]], ALU.is_ge, 0.0, 0, 1, ("ctmp",))
            self.cp("pool", self.triI[:], tmp[:, 0:128], [("ctmp",)], [("triI",)])
            for r in range(4):
                self.memset(tmp[:], 1.0, ("ctmp",))
                self.sel(tmp[:], [[1, 512]], ALU.is_gt, 0.0, -128 * r, -1, ("ctmp",))
                self.cp("pool", self.mask01[r][:], tmp[:], [("ctmp",)], [("mask01", r)])
            P.flush()

    def mixer(self, l):
        P, nc = self.P, self.nc
        w_in_l = self.w["w_in"].ap()[l]
        w_in_v = w_in_l.rearrange("(c p) n -> p c n", p=128)
        with ExitStack() as es:
            A = lambda name, shape, dt: es.enter_context(self.sb(name, shape, dt))
            hT = A("mhT", [128, 8, T], BF16)
            with ExitStack() as es2:
                sq = [es2.enter_context(self.sb("msq%d" % i, [128, 512], BF16)) for i in range(2)]
                rstd = [es2.enter_context(self.sb("mrs%d" % i, [128, 512], F32)) for i in range(2)]
                self.rmsnorm_T(hT, 1, l, 0, T, (sq, rstd))
                P.flush()
            colw = A("colw", [128, 8, 16], F32)
            wba = A("wba", [128, 8, 16], BF16)
            wrep = [A("wrep%d" % i, [128, 8, 128], BF16) for i in range(2)]
            cwraw = A("cwraw", [96, 128], F32)
            cw = A("cw", [128, 96], F32)
            dtb = A("dtb", [128, 8], F32)
            nexpA = A("nexpA", [128, 8], F32)
            gdn = A("gdn", [128, 1], F32)
            gq = A("gq", [128, 1], F32)
            gk = A("gk", [128, 1], F32)
            bcol = A("bcol", [128, 16, 8], F32)
            gcol = A("gcol", [128, 16, 8], F32)
            ngcol = A("ngcol", [128, 16, 8], F32)
            bgc = A("bgc", [128, 16, 8], F32)
            ekd = A("ekd", [128, 16, 8], F32)
            egl = A("egl", [128, 16, 8], F32)
            ctmp = A("lctmp", [128, 16, 8], F32)
            glb = A("glb", [128, 16, 8], F32)
            NWS = 7
            wring = [A("wring%d" % i, [128, 8, 128], BF16) for i in range(NWS)]

            self.dma("sp", colw[:], w_in_v[:, :, 4096:4112], [], [("colw",)])
            self.dma("sp", cwraw[:], self.w["dn_conv_w"].ap()[l].rearrange("i (c p) -> (i c) p", p=128), [], [("cwraw",)])
            self.dma("sp", dtb[:], self.w["dn_dt_bias"].ap()[l].partition_broadcast(128), [], [("dtb",)])
            self.dma("sp", nexpA[:], self.w["dn_a_log"].ap()[l].partition_broadcast(128), [], [("nexpA",)])
            self.dma("sp", gdn[:], self.w["dn_out_norm"].ap()[l].rearrange("(p o) -> p o", o=1), [], [("gdn",)])
            self.dma("sp", gq[:], self.w["sb_q_norm"].ap()[l].rearrange("(p o) -> p o", o=1), [], [("gq",)])
            self.dma("sp", gk[:], self.w["sb_k_norm"].ap()[l].rearrange("(p o) -> p o", o=1), [], [("gk",)])
            self.actf(nexpA[:], nexpA[:], AF.Exp, [("nexpA",)], [("nexpA",)])
            self.ts("dve", nexpA[:], nexpA[:], -1.0, None, ALU.mult, None, [("nexpA",)], [("nexpA",)])
            self.ts("dve", gq[:], gq[:], float(128.0 ** -0.5), None, ALU.mult, None, [("gq",)], [("gq",)])
            self.cp("dve", wba[:], colw[:], [("colw",)], [("wba",)])
            self.tr(self.ps[0][:, 0:96], cwraw[:, :], self.ident[0:96, 0:96], [("cwraw",), ("ident",)], [("ps", 0)])
            self.cp("dve", cw[:], self.ps[0][:, 0:96], [("ps", 0)], [("cw",)])
            for tb in range(16):
                for c in range(8):
                    self.mm(self.ps[1][:, tb * 16:(tb + 1) * 16], hT[:, c, tb * 128:(tb + 1) * 128], wba[:, c, :],
                            c == 0, c == 7, [("hT", c, tb // 4), ("wba",)], [("ps", 1, tb)])
            raw = self.ps[1][:, 0:256].rearrange("p (t q) -> p t q", q=16)
            self.actf(bcol[:], raw[:, :, 0:8], AF.Sigmoid, [("ps", 1)], [("bcol",)])
            self.tt("dve", ctmp[:], raw[:, :, 8:16], dtb[:].unsqueeze(1).to_broadcast([128, 16, 8]), ALU.add,
                    [("ps", 1), ("dtb",)], [("lctmp",)])
            self.actf(ctmp[:], ctmp[:], AF.Exp, [("lctmp",)], [("lctmp",)])
            self.actf(ctmp[:], ctmp[:], AF.Ln, [("lctmp",), ("one1",)], [("lctmp",)], bias=self.one1[:])
            self.tt("dve", ctmp[:], ctmp[:], nexpA[:].unsqueeze(1).to_broadcast([128, 16, 8]), ALU.mult,
                    [("lctmp",), ("nexpA",)], [("lctmp",)])
            c2 = lambda t: t[:].rearrange("p t q -> p (t q)")
            self.mm(self.ps[2][:, 0:128], self.triU[:], c2(ctmp), True, True, [("triU",), ("lctmp",)], [("ps", 2)])
            self.cp("act", c2(gcol), self.ps[2][:, 0:128], [("ps", 2)], [("gcol",)])
            self.padd("act", lambda e: e.mul(out=c2(ngcol), in_=self.ps[2][:, 0:128], mul=-1.0), reads=[("ps", 2)], writes=[("ngcol",)])
            self.mm(self.ps[3][:, 0:128], self.sel127[:], c2(gcol), True, True, [("sel127",), ("gcol",)], [("ps", 3)])
            self.cp("dve", c2(glb), self.ps[3][:, 0:128], [("ps", 3)], [("glb",)])
            self.actf(c2(egl), c2(glb), AF.Exp, [("glb",)], [("egl",)])
            self.tt("dve", c2(ekd), c2(glb), c2(gcol), ALU.subtract, [("glb",), ("gcol",)], [("ekd",)])
            self.actf(c2(ekd), c2(ekd), AF.Exp, [("ekd",)], [("ekd",)])
            self.actf(c2(bgc), c2(gcol), AF.Exp, [("gcol",)], [("bgc",)])
            self.tt("dve", c2(bgc), c2(bgc), c2(bcol), ALU.mult, [("bgc",), ("bcol",)], [("bgc",)])
            if self.debug:
                self.dbg_out("dbg_gcol", c2(gcol), [128, 128], F32, [("gcol",)])
                self.dbg_out("dbg_bcol", c2(bcol), [128, 128], F32, [("bcol",)])
            P.flush()

            def wload(slot, col):
                self.dma("pool", wring[slot][:], w_in_v[:, :, col:col + 128], [], [("wring", slot)])

            DN_COLS = lambda h: [h * 128, 1024 + h * 128, 2048 + h * 128, 3072 + h * 128]
            SB_COLS = lambda h: [4112 + h * 128, 5136 + h * 128, 6160 + h * 128]
            if self.cut == "A":
                return
            if "ffn1" not in self.stages:
                for c in range(8):
                    self.dma("sp", self.x_scr.ap()[c * 128:(c + 1) * 128, :], self.xT[:, c, :], [("xT", c)], [("x_scr", c)])
            P.flush()
            xa = XAlias(self.xT)
            wring2 = [xa.alloc(1024, BF16, row=6 + i // 4, off=(i % 4) * 2048).rearrange("p (k n) -> p k n", n=128) for i in range(NWS)]
            rings = [[w[:] for w in wring], wring2]

            def wload2(ring, slot, col):
                self.dma("pool", rings[ring][slot], w_in_v[:, :, col:col + 128], [], [("wring", ring, slot)])
            for i, col in enumerate(DN_COLS(0) + SB_COLS(0)):
                wload2(0, i, col)
            self._hc = {}
            with ExitStack() as esh:
                def record(gen):
                    rec = []
                    self._rec = rec
                    for _ in gen:
                        pass
                    self._rec = None
                    return rec

                for h in range(self.heads):
                    rg = h % 2
                    xa.reset()
                    rd, rs = [], []
                    if self.cut != "C":
                        rd = record(self.dn_head(esh, xa, l, h, hT, (rg, rings[rg]), colw, wrep, cw, dtb, nexpA, gdn, bcol, gcol, ngcol, bgc, ekd, egl))
                    if self.cut != "B":
                        rs = record(self.sb_head(esh, l, h, hT, (rg, rings[rg]), gq, gk))
                    if h + 1 < self.heads:
                        self._rec = pre = []
                        for i, col in enumerate(DN_COLS(h + 1) + SB_COLS(h + 1)):
                            wload2(1 - rg, i, col)
                        self._rec = None
                        if rd:
                            k = len(rd) // 3
                            rd = rd[:k] + pre + rd[k:]
                        else:
                            k = len(rs) // 3
                            rs = rs[:k] + pre + rs[k:]
                    nd, ns = len(rd), len(rs)
                    i = j = 0
                    while i < nd or j < ns:
                        if j >= ns or (i < nd and i * max(ns, 1) <= j * max(nd, 1)):
                            a, kw = rd[i]
                            i += 1
                        else:
                            a, kw = rs[j]
                            j += 1
                        self.P.add(*a, **kw)
                P.flush()
            xs_v = self.x_scr.ap().rearrange("(c p) t -> p c t", p=128)
            for tg in range(4):
                self.dma("sp", self.xT[:, :, tg * 512:(tg + 1) * 512], xs_v[:, :, tg * 512:(tg + 1) * 512], [("x_scr",)],
                         [("xT", c, tg * 4 + k) for c in range(8) for k in range(4)])
            if self.heads == 8:
                self.mix_out(l, hT)

    def norm_rows(self, src, dst, sq, rtmp, scale, bias_ap, gain_ap, banks, key_src, key_dst, kp=""):
        pb = self.ps[banks]
        self.actf(sq[:], src, AF.Square, [key_src], [(kp + "nsq",)])
        self.mm(pb[:], self.ones[:], sq[:], True, True, [(kp + "nsq",), ("ones",)], [("ps", banks)])
        self.actf(rtmp[:], pb[:], AF.Ln, [("ps", banks)], [(kp + "nrt",)], bias=bias_ap, scale=scale)
        self.actf(rtmp[:], rtmp[:], AF.Exp, [(kp + "nrt",)], [(kp + "nrt",)], scale=-0.5)
        if gain_ap is None:
            self.tt("dve", dst, src, rtmp[:], ALU.mult, [key_src, (kp + "nrt",)], [key_dst])
        else:
            self.stt("dve", dst, src, gain_ap, rtmp[:], ALU.mult, ALU.mult, [key_src, (kp + "nrt",)], [key_dst])

    def dn_head(self, es, xa, l, h, hT, wr, colw, wrep, cw, dtb, nexpA, gdn, bcol, gcol, ngcol, bgc, ekd, egl):
        P, nc = self.P, self.nc
        rg, wring = wr
        if True:
            def A(name, shape, dt):
                k = ("dn", name)
                if k not in self._hc:
                    self._hc[k] = es.enter_context(self.sb(name, shape, dt))
                return self._hc[k]
            B1 = [A("B1_%d" % i, [128, 515], F32) for i in range(3)]
            B2 = [A("B2_%d" % i, [128, 512], F32) for i in range(3)]
            QT = A("QT", [128, 512], BF16)
            KT = A("KT", [128, 512], BF16)
            VT = A("VT", [128, 512], BF16)
            g_row = A("g_row", [128, 512], F32)
            R1 = A("R1", [128, 512], F32)
            R2 = A("R2", [128, 512], F32)
            betaSU = A("betaSU", [128, 512], BF16)
            qdecT_ = [xa.alloc(512, BF16) for _ in range(2)]
            kbg = A("kbg", [128, 512], BF16)
            kdec_ = [xa.alloc(512, BF16) for _ in range(2)]
            bv_ = [xa.alloc(512, BF16) for _ in range(2)]
            Tt_ = [xa.alloc(512, BF16) for _ in range(2)]
            attnT_ = [xa.alloc(512, BF16) for _ in range(2)]
            nwT_ = [xa.alloc(512, BF16) for _ in range(2)]
            zs_ = [xa.alloc(512, BF16) for _ in range(2)]
            oTg = xa.alloc(512, F32)
            R3 = xa.alloc(512, F32)
            R2o = xa.alloc(512, F32)
            sq_o = xa.alloc(512, BF16)
            rtmp_o = xa.alloc(512, F32)
            oab = [A("oab%d" % i, [128, 512], BF16) for i in range(1)]
            sq = A("nsq", [128, 512], BF16)
            rtmp = A("nrt", [128, 512], F32)
            tmp1 = [xa.alloc(128, F32) for _ in range(4)]
            tmp2 = [xa.alloc(128, F32) for _ in range(4)]
            DT = [xa.alloc(128, F32) for _ in range(4)]
            Dm = [xa.alloc(128, F32) for _ in range(4)]
            t1 = [xa.alloc(128, F32) for _ in range(4)]
            Mb = [A("Mb_%d" % i, [128, 512], BF16) for i in range(2)]
            Mtb = [A("Mtb_%d" % i, [128, 512], BF16) for i in range(2)]
            C1 = A("C1", [128, 512], BF16)
            Ct1 = A("Ct1", [128, 512], BF16)
            C2 = A("C2", [128, 512], BF16)
            Ct2 = A("Ct2", [128, 512], BF16)
            X = A("X", [128, 512], BF16)
            Yb = A("Yb", [128, 512], BF16)
            Ytb = A("Ytb", [128, 512], BF16)
            S = A("S", [128, 128], F32)
            Sbf = [A("Sbf_%d" % i, [128, 128], BF16) for i in range(2)]
            vnew = [A("vnew_%d" % i, [128, 128], BF16) for i in range(2)]

            for n in range(3):
                self.memset(B1[n][:, 0:3], 0.0, ("B1", n, "halo"))
            self.memset(S[:], 0.0, ("S",))
            self.memset(Sbf[0][:], 0.0, ("Sbf", 0))
            for qi, colidx in enumerate((h, 8 + h)):
                self.cp("dve", wrep[qi][:], colw[:, :, colidx:colidx + 1].to_broadcast([128, 8, 128]),
                        [("colw",)], [("wrep", qi)])
            nb = [0]

            def bank4():
                nb[0] = (nb[0] + 1) % 2
                return nb[0]

            def record_sub(gen):
                save = self._rec
                self._rec = sub = []
                for _ in gen:
                    pass
                self._rec = save
                return sub

            def emit_ops(ops):
                for a, kw in ops:
                    self.padd(*a, **kw)

            def prep(tg):
                a = tg * 512
                t2 = tg % 2
                qdecT, kdec, bv, Tt, attnT, nwT, zs = qdecT_[t2], kdec_[t2], bv_[t2], Tt_[t2], attnT_[t2], nwT_[t2], zs_[t2]
                def proj(n):
                    b = n
                    for c in range(8):
                        self.mm(self.ps[b][:], wring[n][:, c, :], hT[:, c, a:a + 512], c == 0, c == 7,
                                [("wring", rg, n), ("hT", c, tg)], [("ps", b)])
                    self.cp("act", B1[n][:, 3:515], self.ps[b][:], [("ps", b)], [("B1", n, "main")])
                    acc = B2[n]
                    ch = n * 8 + h
                    self.ts("dve", acc[:], B1[n][:, 3:515], cw[:, 3 * 24 + ch:3 * 24 + ch + 1], None, ALU.mult, None,
                            [("B1", n, "main"), ("cw",)], [("B2", n)])
                    for i in (2, 1, 0):
                        self.stt("dve", acc[:], B1[n][:, i:i + 512], cw[:, i * 24 + ch:i * 24 + ch + 1], acc[:],
                                 ALU.mult, ALU.add, [("B1", n), ("cw",), ("B2", n)], [("B2", n)])
                    if tg < 3:
                        self.cp("pool", B1[n][:, 0:3], B1[n][:, 512:515], [("B1", n, "main")], [("B1", n, "halo")])
                    yield

                emit_ops(merge_n([record_sub(proj(n)) for n in range(3)]))
                bzz = 0
                for c in range(8):
                    self.mm(self.ps[bzz][:], wring[3][:, c, :], hT[:, c, a:a + 512], c == 0, c == 7,
                            [("wring", rg, 3), ("hT", c, tg)], [("ps", bzz)])
                self.actf(B2[0][:], B2[0][:], AF.Silu, [("B2", 0)], [("B2", 0)])
                self.actf(B2[1][:], B2[1][:], AF.Silu, [("B2", 1)], [("B2", 1)])
                self.actf(VT[:], B2[2][:], AF.Silu, [("B2", 2)], [("VT",)])
                self.actf(zs[:], self.ps[bzz][:], AF.Silu, [("ps", bzz)], [("zs", t2)])
                for n in range(2):
                    dst = QT if n == 0 else KT
                    self.norm_rows(B2[n][:], dst[:], sq, rtmp, 128.0 if n == 0 else 1.0,
                                   self.eps128[:] if n == 0 else self.epsb[:], None, 2,
                                   ("B2", n), ("QT",) if n == 0 else ("KT",), kp="dn")
                yield

                def rowb():
                    bb = 0
                    for c in range(8):
                        self.mm(self.ps[bb][:], wrep[0][:, c, :], hT[:, c, a:a + 512], c == 0, c == 7,
                                [("wrep", 0), ("hT", c, tg)], [("ps", bb)])
                    self.actf(R1[:], self.ps[bb][:], AF.Exp, [("ps", bb)], [("R1",)], scale=-1.0)
                    self.actf(R1[:], R1[:], AF.Ln, [("R1",), ("one1",)], [("R1",)], bias=self.one1[:])
                    self.actf(R1[:], R1[:], AF.Exp, [("R1",)], [("R1",)], scale=-1.0)
                    self.tt("dve", betaSU[:].rearrange("p (c j) -> p c j", j=128), R1[:].rearrange("p (c j) -> p c j", j=128),
                            self.SU[:].unsqueeze(1).to_broadcast([128, 4, 128]), ALU.mult, [("R1",), ("SU",)], [("betaSU",)])
                    yield

                def rowg():
                    ba = 1
                    for c in range(8):
                        self.mm(self.ps[ba][:], wrep[1][:, c, :], hT[:, c, a:a + 512], c == 0, c == 7,
                                [("wrep", 1), ("hT", c, tg)], [("ps", ba)])
                    self.actf(R2[:], self.ps[ba][:], AF.Exp, [("ps", ba), ("dtb",)], [("R2",)], bias=dtb[:, h:h + 1])
                    self.actf(R2[:], R2[:], AF.Ln, [("R2",), ("one1",)], [("R2",)], bias=self.one1[:])
                    self.ts("dve", R2[:], R2[:], nexpA[:, h:h + 1], None, ALU.mult, None, [("R2",), ("nexpA",)], [("R2",)])
                    self.padd("dve", lambda e: e.tensor_tensor_scan(out=g_row[:], data0=self.cmask[:], data1=R2[:], initial=0.0,
                                                                    op0=ALU.mult, op1=ALU.add),
                              reads=[("R2",), ("cmask",)], writes=[("g_row",)])
                    self.actf(R3[:], g_row[:], AF.Exp, [("g_row",)], [("R3",)])
                    self.tt("dve", qdecT[:], QT[:], R3[:], ALU.mult, [("QT",), ("R3",)], [("qdecT", t2)])
                    yield

                emit_ops(merge_n([record_sub(rowb()), record_sub(rowg())]))
                psb2 = self.ps[2].bitcast(BF16)

                def chunk(j):
                    c = tg * 4 + j
                    js = slice(j * 128, (j + 1) * 128)
                    self.tr(psb2[:, j * 256:j * 256 + 128], KT[:, js], self.identb[:], [("KT",), ("identb",)], [("ps", 2)])
                    self.tr(psb2[:, j * 256 + 128:j * 256 + 256], VT[:, js], self.identb[:], [("VT",), ("identb",)], [("ps", 2)])
                    self.actf(kbg[:, js], psb2[:, j * 256:j * 256 + 128], AF.Copy, [("ps", 2), ("bgc",)], [("kbg", j)],
                              scale=bgc[:, c, h:h + 1])
                    self.ts("dve", kdec[:, js], psb2[:, j * 256:j * 256 + 128], ekd[:, c, h:h + 1], None, ALU.mult, None,
                            [("ps", 2), ("ekd",)], [("kdec", t2, j)])
                    self.actf(bv[:, js], psb2[:, j * 256 + 128:j * 256 + 256], AF.Copy, [("ps", 2), ("bcol",)], [("bv", t2, j)],
                              scale=bcol[:, c, h:h + 1])
                    pb = j // 2
                    pa = self.ps[pb]
                    c0 = (j % 2) * 256
                    self.mm(pa[:, c0:c0 + 128], KT[:, js], KT[:, js], True, True, [("KT",)], [("ps", pb)])
                    self.mm(pa[:, c0 + 128:c0 + 256], KT[:, js], QT[:, js], True, True, [("KT",), ("QT",)], [("ps", pb)])
                    k2 = j
                    self.tt("pool", tmp1[k2][:], g_row[:, js], self.NEGlo[:], ALU.add, [("g_row",), ("NEGlo",)], [("tmp1", k2)])
                    self.actf(DT[k2][:], tmp1[k2][:], AF.Exp, [("tmp1", k2), ("ngcol",)], [("DT", k2)], bias=ngcol[:, c, h:h + 1])
                    self.tt("pool", tmp2[k2][:], self.NEGui[:], g_row[:, js], ALU.subtract, [("g_row",), ("NEGui",)], [("tmp2", k2)])
                    self.actf(Dm[k2][:], tmp2[k2][:], AF.Exp, [("tmp2", k2), ("gcol",)], [("Dm", k2)], bias=gcol[:, c, h:h + 1])
                    self.tt("dve", t1[k2][:], pa[:, c0:c0 + 128], DT[k2][:], ALU.mult, [("ps", pb), ("DT", k2)], [("t1", k2)])
                    self.tt("pool", Mtb[0][:, js], t1[k2][:], betaSU[:, js], ALU.mult, [("t1", k2), ("betaSU",)], [("Mtb", 0, j)])
                    self.stt("dve", Mb[0][:, js], pa[:, c0:c0 + 128], bcol[:, c, h:h + 1], Dm[k2][:], ALU.mult, ALU.mult,
                             [("ps", pb), ("bcol",), ("Dm", k2)], [("Mb", 0, j)])
                    self.tt("dve", attnT[:, js], pa[:, c0 + 128:c0 + 256], DT[k2][:], ALU.mult, [("ps", pb), ("DT", k2)], [("attnT", t2, j)])
                    yield

                emit_ops(merge_n([record_sub(chunk(j)) for j in range(4)]))
                yield
                v3 = lambda t: t[:].rearrange("p (c j) -> p c j", j=128)
                mb3 = lambda m: m[:].unsqueeze(1).to_broadcast([128, 4, 128])
                Lf, Ltf = Mb[0], Mtb[0]
                self.tt("dve", v3(C1), v3(Lf), mb3(self.mB), ALU.mult, [("Mb", 0), ("mB",)], [("C1",)])
                self.tt("dve", v3(Ct1), v3(Ltf), mb3(self.mB), ALU.mult, [("Mtb", 0), ("mB",)], [("Ct1",)])
                self.tt("dve", v3(C2), v3(Lf), mb3(self.mC), ALU.mult, [("Mb", 0), ("mC",)], [("C2",)])
                self.tt("dve", v3(Ct2), v3(Ltf), mb3(self.mC), ALU.mult, [("Mtb", 0), ("mC",)], [("Ct2",)])
                self.tt("dve", v3(Lf), v3(Lf), mb3(self.mA), ALU.mult, [("Mb", 0), ("mA",)], [("Mb", 0)])
                self.tt("dve", v3(Ltf), v3(Ltf), mb3(self.mA), ALU.mult, [("Mtb", 0), ("mA",)], [("Mtb", 0)])
                self.tt("dve", v3(X), mb3(self.identb), v3(Lf), ALU.subtract, [("Mb", 0), ("identb",)], [("X",)])
                self.tt("dve", v3(Tt), mb3(self.identb), v3(Ltf), ALU.subtract, [("Mtb", 0), ("identb",)], [("Tt", t2)])
                J4 = [slice(j * 128, (j + 1) * 128) for j in range(4)]
                for k in range(1, 5):
                    yield
                    pv_, cu = (k - 1) % 2, k % 2
                    for js in J4:
                        self.mm(self.ps[0][:, js], Mtb[pv_][:, js], Mb[pv_][:, js], True, True, [("Mtb", pv_), ("Mb", pv_)], [("ps", 0)])
                    for js in J4:
                        self.mm(self.ps[1][:, js], Mb[pv_][:, js], Mtb[pv_][:, js], True, True, [("Mtb", pv_), ("Mb", pv_)], [("ps", 1)])
                    self.cp("act", Mb[cu][:], self.ps[0][:], [("ps", 0)], [("Mb", cu)])
                    yield
                    self.cp("dve", Mtb[cu][:], self.ps[1][:], [("ps", 1)], [("Mtb", cu)])
                    yield
                    for js in J4:
                        self.mm(self.ps[0][:, js], Mtb[cu][:, js], X[:, js], True, True, [("Mtb", cu), ("X",)], [("ps", 0)])
                    for js in J4:
                        self.mm(self.ps[1][:, js], Mb[cu][:, js], Tt[:, js], True, True, [("Mb", cu), ("Tt", t2)], [("ps", 1)])
                    yield
                    self.tt("dve", X[:], X[:], self.ps[0][:], ALU.add, [("X",), ("ps", 0)], [("X",)])
                    self.tt("dve", Tt[:], Tt[:], self.ps[1][:], ALU.add, [("Tt", t2), ("ps", 1)], [("Tt", t2)])
                yield
                for js in J4:
                    self.mm(self.ps[0][:, js], Ct1[:, js], X[:, js], True, True, [("Ct1",), ("X",)], [("ps", 0)])
                for js in J4:
                    self.mm(self.ps[1][:, js], C1[:, js], Tt[:, js], True, True, [("C1",), ("Tt", t2)], [("ps", 1)])
                yield
                self.cp("act", Yb[:], self.ps[0][:], [("ps", 0)], [("Yb",)])
                self.cp("dve", Ytb[:], self.ps[1][:], [("ps", 1)], [("Ytb",)])
                for js in J4:
                    self.mm(self.ps[0][:, js], Tt[:, js], Yb[:, js], True, True, [("Tt", t2), ("Yb",)], [("ps", 0)])
                for js in J4:
                    self.mm(self.ps[1][:, js], X[:, js], Ytb[:, js], True, True, [("X",), ("Ytb",)], [("ps", 1)])
                yield
                self.tt("dve", X[:], X[:], self.ps[0][:], ALU.subtract, [("X",), ("ps", 0)], [("X",)])
                self.tt("dve", Tt[:], Tt[:], self.ps[1][:], ALU.subtract, [("Tt", t2), ("ps", 1)], [("Tt", t2)])
                yield
                for js in J4:
                    self.mm(self.ps[1][:, js], C2[:, js], Tt[:, js], True, True, [("C2",), ("Tt", t2)], [("ps", 1)])
                yield
                self.cp("dve", Ytb[:], self.ps[1][:], [("ps", 1)], [("Ytb",)])
                yield
                for js in J4:
                    self.mm(self.ps[1][:, js], X[:, js], Ytb[:, js], True, True, [("X",), ("Ytb",)], [("ps", 1)])
                self.tt("dve", Tt[:], Tt[:], self.ps[1][:], ALU.subtract, [("Tt", t2), ("ps", 1)], [("Tt", t2)])
                for j in range(4):
                    js = slice(j * 128, (j + 1) * 128)
                    self.mm(self.ps[0][:, js], kbg[:, js], Tt[:, js], True, True, [("kbg", j), ("Tt", t2, j)], [("ps", 0)])
                self.padd("act", lambda e: e.mul(out=nwT[:], in_=self.ps[0][:], mul=-1.0), reads=[("ps", 0)], writes=[("nwT", t2)])
            def recur(tg):
                a = tg * 512
                t2 = tg % 2
                qdecT, kdec, bv, Tt, attnT, nwT, zs = qdecT_[t2], kdec_[t2], bv_[t2], Tt_[t2], attnT_[t2], nwT_[t2], zs_[t2]
                po = self.ps[3]
                for j in range(4):
                    c = tg * 4 + j
                    js = slice(j * 128, (j + 1) * 128)
                    yield
                    sb_cur, sb_nxt = Sbf[c % 2], Sbf[(c + 1) % 2]
                    vn = vnew[c % 2]
                    pv = self.ps[7]
                    self.mm(pv[:, 0:128], Tt[:, js], bv[:, js], True, False, [("Tt", t2, j), ("bv", t2, j)], [("ps", 7)])
                    self.mm(pv[:, 0:128], nwT[:, js], sb_cur[:], False, True, [("nwT", t2), ("Sbf", c % 2)], [("ps", 7)])
                    yield
                    self.cp("act", vn[:], pv[:, 0:128], [("ps", 7)], [("vnew", c % 2)])
                    yield
                    self.mm(po[:, js], sb_cur[:], qdecT[:, js], True, False, [("Sbf", c % 2), ("qdecT", t2)], [("ps", 3)])
                    self.mm(po[:, js], vn[:], attnT[:, js], False, True, [("vnew", c % 2), ("attnT", t2, j)], [("ps", 3)])
                    pS = self.ps[7]
                    self.mm(pS[:, 0:128], kdec[:, js], vn[:], True, True, [("kdec", t2, j), ("vnew", c % 2)], [("ps", 7)])
                    yield
                    self.stt("dve", S[:], S[:], egl[:, c, h:h + 1], pS[:, 0:128], ALU.mult, ALU.add,
                             [("S",), ("egl",), ("ps", 7)], [("S",)])
                    yield
                    self.cp("act", sb_nxt[:], S[:], [("S",)], [("Sbf", (c + 1) % 2)])
                self.cp("act", oTg[:], po[:], [("ps", 3)], [("oTg",)])
                yield
                self.norm_rows(oTg[:], R2o[:], sq_o, rtmp_o, 1.0 / 128.0, self.epsb[:], gdn[:, 0:1], 7, ("oTg",), ("R2o",), kp="dno")
                ob = oab[0]
                self.tt("dve", ob[:], R2o[:], zs[:], ALU.mult, [("R2o",), ("zs", t2)], [("oab", 0)])
                self.dma("sp", self.oa_scr.ap()[h * 128:(h + 1) * 128, a:a + 512], ob[:], [("oab", 0)], [("oa_scr", h, tg)])

            pend = None
            for tg in range(4):
                rp = record_sub(prep(tg))
                if pend is None:
                    emit_ops(rp)
                else:
                    emit_ops(merge_ops(rp, pend))
                pend = record_sub(recur(tg))
            emit_ops(pend)
            yield

    def sb_head(self, es, l, h, hT, wr, gq, gk):
        P, nc = self.P, self.nc
        rg, wring = wr
        if True:
            def A(name, shape, dt):
                k = ("sb", name)
                if k not in self._hc:
                    self._hc[k] = es.enter_context(self.sb(name, shape, dt))
                return self._hc[k]
            kn = A("kn", [128, T], BF16)
            qn = [A("qn%d" % i, [128, 512], BF16) for i in range(2)]
            Vtok = A("Vtok", [128, T], BF16)
            raw = [A("sraw%d" % i, [128, 512], F32) for i in range(1)]
            sq = A("nsq", [128, 512], BF16)
            rtmp = A("nrt", [128, 512], F32)
            e_t = [A("e_t%d" % i, [128, 512], BF16) for i in range(5)]
            spb = [A("spb%d" % i, [128, 512], BF16) for i in range(3)]
            E2 = [A("E2_%d" % i, [128, 512], BF16) for i in range(2)]
            Wt = [A("Wt%d" % i, [128, 512], BF16) for i in range(2)]
            runb = [A("runb%d" % i, [128, 512], BF16) for i in range(2)]
            obb = [A("obb%d" % i, [128, 512], BF16) for i in range(1)]
            def proj_qk(n, tg):
                a = tg * 512
                b = 4
                for c in range(8):
                    self.mm(self.ps[b][:], wring[4 + n][:, c, :], hT[:, c, a:a + 512], c == 0, c == 7,
                            [("wring", rg, 4 + n), ("hT", c, tg)], [("ps", b)])
                r = raw[0]
                self.cp("act", r[:], self.ps[b][:], [("ps", b)], [("sraw", 0)])
                if n == 0:
                    self.norm_rows(r[:], qn[tg % 2][:], sq, rtmp, 1.0 / 128.0, self.epsb[:], gq[:, 0:1], 4, ("sraw", 0), ("qn", tg % 2), kp="sb")
                else:
                    self.norm_rows(r[:], kn[:, a:a + 512], sq, rtmp, 1.0 / 128.0, self.epsb[:], gk[:, 0:1], 4, ("sraw", 0), ("kn", tg), kp="sb")

            for tg in range(4):
                proj_qk(1, tg)
                yield
            for tbg in range(4):
                b = 4 + tbg % 2
                for jj in range(4):
                    tb = tbg * 4 + jj
                    for c in range(8):
                        self.mm(self.ps[b][:, jj * 128:(jj + 1) * 128], hT[:, c, tb * 128:(tb + 1) * 128], wring[6][:, c, :],
                                c == 0, c == 7, [("wring", rg, 6), ("hT", c, tbg)], [("ps", b, jj)])
                self.cp("act", Vtok[:, tbg * 512:(tbg + 1) * 512], self.ps[b][:], [("ps", b)], [("Vtok", tbg)])
                yield
            if self.debug and h == 0:
                self.dbg_out("dbg_kn", kn[:], [128, T], BF16, [("kn",)])
                self.dbg_out("dbg_Vtok", Vtok[:], [128, T], BF16, [("Vtok",)])
            pairs = []
            for qg in range(4):
                nkb = 4 * qg + 4
                for idx, kb in enumerate(range(nkb - 1, -1, -1)):
                    pairs.append((qg, idx, kb, nkb))
            NP = len(pairs)

            def op_mmz(p):
                qg, idx, kb, nkb = pairs[p]
                if idx == 0:
                    proj_qk(0, qg)
                ks = slice(kb * 128, (kb + 1) * 128)
                self.mm(self.ps[4][:], kn[:, ks], qn[qg % 2][:], True, True, [("kn", kb // 4), ("qn", qg % 2)], [("ps", 4)])

            def op_exp(p):
                qg, idx, kb, nkb = pairs[p]
                e5 = p % 5
                self.actf(e_t[e5][:], self.ps[4][:], AF.Exp, [("ps", 4)], [("e_t", e5)])
                r = kb - 4 * qg
                if r >= 0:
                    self.tt("dve", e_t[e5][:], e_t[e5][:], self.mask01[r][:], ALU.mult, [("e_t", e5), ("mask01", r)], [("e_t", e5)])

            def op_ln(p):
                e5, e3 = p % 5, p % 3
                self.actf(spb[e3][:], e_t[e5][:], AF.Ln, [("e_t", e5), ("one1",)], [("spb", e3)], bias=self.one1[:])

            def op_smm(p):
                qg, idx, kb, nkb = pairs[p]
                e3 = p % 3
                pS = self.ps[5]
                self.mm(pS[:], self.triI[:], spb[e3][:], True, idx == 0, [("triI",), ("spb", e3)], [("ps", 5)])
                if idx >= 1:
                    self.mm(pS[:], self.ones[:], runb[(idx - 1) % 2][:], False, True, [("ones",), ("runb", (idx - 1) % 2)], [("ps", 5)])
                if kb > 0:
                    if idx == 0:
                        self.cp("dve", runb[0][:], spb[e3][:], [("spb", e3)], [("runb", 0)])
                    else:
                        self.tt("dve", runb[idx % 2][:], runb[(idx - 1) % 2][:], spb[e3][:], ALU.add,
                                [("runb", (idx - 1) % 2), ("spb", e3)], [("runb", idx % 2)])

            def op_e2(p):
                self.actf(E2[p % 2][:], self.ps[5][:], AF.Exp, [("ps", 5)], [("E2", p % 2)], scale=-1.0)

            def op_w(p):
                e5 = p % 5
                self.tt("dve", Wt[p % 2][:], e_t[e5][:], E2[p % 2][:], ALU.mult, [("e_t", e5), ("E2", p % 2)], [("Wt", p % 2)])

            def op_pv(p):
                qg, idx, kb, nkb = pairs[p]
                qa = qg * 512
                ks = slice(kb * 128, (kb + 1) * 128)
                po = self.ps[6]
                self.mm(po[:], Vtok[:, ks], Wt[p % 2][:], idx == 0, idx == nkb - 1, [("Vtok", kb // 4), ("Wt", p % 2)], [("ps", 6)])
                if idx == nkb - 1:
                    ob = obb[0]
                    self.cp("act", ob[:], po[:], [("ps", 6)], [("obb", 0)])
                    self.dma("sp", self.ob_scr.ap()[h * 128:(h + 1) * 128, qa:qa + 512], ob[:], [("obb", 0)], [("ob_scr", h, qg)])

            stages = [(op_pv, 6), (op_w, 5), (op_e2, 4), (op_smm, 3), (op_ln, 2), (op_exp, 1), (op_mmz, 0)]
            for t in range(NP + 6):
                for fn, d in stages:
                    p = t - d
                    if 0 <= p < NP:
                        fn(p)
                yield

    def mix_out(self, l, hT):
        P, nc = self.P, self.nc
        wa_v = self.w["w_branch_a"].ap()[l].rearrange("(c p) n -> p c n", p=128)
        wb_v = self.w["w_branch_b"].ap()[l].rearrange("(c p) n -> p c n", p=128)
        wo_v = self.w["w_out"].ap()[l].rearrange("(c p) n -> p c n", p=128)
        w_in_v = self.w["w_in"].ap()[l].rearrange("(c p) n -> p c n", p=128)
        oa_v = self.oa_scr.ap().rearrange("(c p) t -> p c t", p=128)
        ob_v = self.ob_scr.ap().rearrange("(c p) t -> p c t", p=128)
        with ExitStack() as es:
            A = lambda name, shape, dt: es.enter_context(self.sb(name, shape, dt))
            NW = 10
            wr = [A("owr%d" % i, [128, 8, 128], BF16) for i in range(NW)]
            oat = [A("oat%d" % i, [128, 8, 512], BF16) for i in range(2)]
            obt = [A("obt%d" % i, [128, 8, 512], BF16) for i in range(2)]
            merged = A("merged", [128, 8, 512], BF16)
            sa = [A("sga%d" % i, [128, 512], F32) for i in range(2)]
            sb = [A("sgb%d" % i, [128, 512], F32) for i in range(2)]
            m1 = [A("m1_%d" % i, [128, 512], F32) for i in range(2)]
            m2 = [A("m2_%d" % i, [128, 512], F32) for i in range(2)]
            sched = []
            for tg in range(4):
                for c in range(8):
                    sched.append((wa_v, c * 128))
                    sched.append((wb_v, c * 128))
                    sched.append((w_in_v, 7184 + c * 128))
                    sched.append((w_in_v, 8208 + c * 128))
                for c2 in range(8):
                    sched.append((wo_v, c2 * 128))
            nload = [0]

            def issue():
                i = nload[0]
                if i < len(sched):
                    src, col = sched[i]
                    self.dma("pool", wr[i % NW][:], src[:, :, col:col + 128], [], [("owr", i % NW)])
                    nload[0] += 1

            for _ in range(NW - 2):
                issue()
            wi = 0
            for tg in range(4):
                a = tg * 512
                self.dma("sp", oat[tg % 2][:], oa_v[:, :, a:a + 512], [("oa_scr",)], [("oat", tg % 2)])
                self.dma("sp", obt[tg % 2][:], ob_v[:, :, a:a + 512], [("ob_scr",)], [("obt", tg % 2)])
                for c in range(8):
                    s4 = c % 2
                    banks = [0 + 4 * s4, 1 + 4 * s4, 2 + 4 * s4, 3 + 4 * s4]
                    srcs = [(oat[tg % 2], ("oat", tg % 2)), (obt[tg % 2], ("obt", tg % 2)), None, None]
                    for q in range(4):
                        slot = wi % NW
                        wi += 1
                        issue()
                        for k in range(8):
                            if q < 2:
                                rhs, rk = srcs[q][0][:, k, :], srcs[q][1]
                            else:
                                rhs, rk = hT[:, k, a:a + 512], ("hT", k, tg)
                            self.mm(self.ps[banks[q]][:], wr[slot][:, k, :], rhs, k == 0, k == 7, [("owr", slot), rk], [("ps", banks[q])])
                    self.actf(sa[s4][:], self.ps[banks[2]][:], AF.Sigmoid, [("ps", banks[2])], [("sga", s4)])
                    self.actf(sb[s4][:], self.ps[banks[3]][:], AF.Sigmoid, [("ps", banks[3])], [("sgb", s4)])
                    self.tt("dve", m1[s4][:], sa[s4][:], self.ps[banks[0]][:], ALU.mult, [("sga", s4), ("ps", banks[0])], [("m1", s4)])
                    self.tt("dve", m2[s4][:], sb[s4][:], self.ps[banks[1]][:], ALU.mult, [("sgb", s4), ("ps", banks[1])], [("m2", s4)])
                    self.tt("pool", merged[:, c, :], m1[s4][:], m2[s4][:], ALU.add, [("m1", s4), ("m2", s4)], [("merged", c)])
                for c2 in range(8):
                    slot = wi % NW
                    wi += 1
                    issue()
                    b = c2 % 8
                    for k in range(8):
                        self.mm(self.ps[b][:], wr[slot][:, k, :], merged[:, k, :], k == 0, k == 7, [("owr", slot), ("merged", k)], [("ps", b)])
                    xs = self.xT[:, c2, a:a + 512]
                    self.tt("dve", xs, xs, self.ps[b][:], ALU.add,
                            [("ps", b)] + [("xT", c2, a // 128 + kk) for kk in range(4)],
                            [("xT", c2, a // 128 + kk) for kk in range(4)])
            P.flush()

    def dbg_out(self, name, ap, shape, dt, R):
        t = self.nc.dram_tensor(name, list(shape), dt, kind="ExternalOutput")
        self.dbg_names.append(name)
        self.dma("sp", t.ap(), ap, R, [("dbg", name)])


_CACHE = {}


def _get_nc(layers, stages, heads=8, debug=False):
    key = (layers, tuple(stages), heads, debug)
    if key not in _CACHE:
        b = Builder(layers=layers, stages=stages, heads=heads, debug=debug)
        _CACHE[key] = (b.build(), b)
    return _CACHE[key][0]


def run(inputs, layers=DEPTH, stages=("ffn1", "mix", "ffn2"), trace=False, heads=8, debug=False):
    nc = _get_nc(layers, stages, heads, debug)
    x = np.ascontiguousarray(inputs["x"], dtype=np.float32)
    B = x.shape[0]
    in_maps = []
    for b in range(B):
        m = {"x": x[b]}
        for k, v in inputs.items():
            if k != "x":
                m[k] = np.ascontiguousarray(v, dtype=np.float32)
        in_maps.append(m)
    res = run_bass_kernel_spmd(nc, in_maps, core_ids=list(range(B)), trace=trace)
    out = np.stack([np.asarray(r["out"]) for r in res.results], axis=0)
    return out, res


def kernel(**inputs):
    out, _ = run(inputs)
    return out.astype(np.float32)
```
